# Optimizing a Trainium2 kernel written in Bass

```python
import jax, jax.numpy as jnp
from jax import lax
import numpy as np

D_MODEL = 2048
BATCH = 4
SEQ = 2048
DEPTH = 4
DEC_BATCH = 128
DEC_SEQ = 8
PAST_LEN = 16384
PAGE_SIZE = 128

W_A = D_MODEL // 2
HEAD_A = 64
H_A = W_A // HEAD_A
LORA_W = 64
LORA_A = 64
W_B = D_MODEL // 2
DK_B = 128
H_B = W_B // DK_B
DV_B = W_B // H_B
SHIFT_W = 3 * W_A + LORA_W + LORA_A
OFF_ZA = SHIFT_W
OFF_QB = OFF_ZA + W_A
OFF_FB = OFF_QB + W_B
OFF_IB = OFF_FB + W_B
OFF_ZB = OFF_IB + W_B
OFF_GA = OFF_ZB + W_B
OFF_GB = OFF_GA + D_MODEL
P_TOTAL = OFF_GB + D_MODEL
GLA_CHUNK = 16
NORM_EPS = 1e-6
GN_EPS = 64e-5
MAX_INPUT_GATE = 1.0 - 1e-6

kernel_name = "rwkv7_hgrn2_gated_hybrid_step"


def rms_norm(x, g, eps=NORM_EPS):
    xf = x.astype(jnp.float32)
    y = xf * lax.rsqrt(jnp.mean(xf * xf, axis=-1, keepdims=True) + eps)
    return (y * g.astype(jnp.float32)).astype(x.dtype)


def rwkv7_recurrence(r, w, k, v, kk, a, s0):
    def step(s, inp):
        r_t, w_t, k_t, v_t, kk_t, a_t = inp
        sa = jnp.einsum('nhvk,nhk->nhv', s, kk_t)
        s = (s * w_t[:, :, None, :]
             - sa[..., None] * (kk_t * a_t)[:, :, None, :]
             + v_t[..., None] * k_t[:, :, None, :])
        return s, jnp.einsum('nhvk,nhk->nhv', s, r_t)
    xs = tuple(jnp.swapaxes(t, 0, 1) for t in (r, w, k, v, kk, a))
    s_end, ys = lax.scan(step, s0.astype(jnp.float32), xs)
    return jnp.swapaxes(ys, 0, 1), s_end


def gla_chunkwise(q, k, v, log_g, s0):
    n, h, L, _ = q.shape
    c = min(GLA_CHUNK, L)
    n_chunks = -(-L // c)
    pad = n_chunks * c - L
    if pad:
        cfg = ((0, 0), (0, 0), (0, pad), (0, 0))
        q, k, v, log_g = tuple(jnp.pad(t, cfg) for t in (q, k, v, log_g))

    def to_chunks(t):
        return jnp.moveaxis(t.reshape(n, h, n_chunks, c, t.shape[-1]), 2, 0)

    causal = jnp.tril(jnp.ones((c, c), dtype=bool))[:, :, None]

    def step(s, inp):
        qc, kc, vc, gc = inp
        b = jnp.cumsum(gc, axis=2)
        o = jnp.einsum('nhtd,nhde->nhte', qc * jnp.exp(b), s)
        diff = b[:, :, :, None, :] - b[:, :, None, :, :]
        decay_ts = jnp.where(causal, jnp.exp(jnp.minimum(diff, 0.0)), 0.0)
        att = jnp.einsum('nhtd,nhsd,nhtsd->nhts', qc, kc, decay_ts)
        o = o + jnp.einsum('nhts,nhse->nhte', att, vc)
        b_last = b[:, :, -1:, :]
        s = (jnp.exp(b_last[:, :, 0, :])[..., None] * s
             + jnp.einsum('nhsd,nhse->nhde', kc * jnp.exp(b_last - b), vc))
        return s, o

    s_end, outs = lax.scan(step, s0.astype(jnp.float32),
                           tuple(to_chunks(t) for t in (q, k, v, log_g)))
    o = jnp.moveaxis(outs, 0, 2).reshape(n, h, n_chunks * c, -1)[:, :, :L]
    return o, s_end


def mixer_layer(x, prev_row, s_a, s_b, lb, norm_g, w_in, shift_mu, w0, w2, a0, a2,
                k_k, k_a, r_k, ln_w, ln_b, hg_g, proj_a, proj_b, w_out):
    f32 = jnp.float32
    n, L, _ = x.shape
    h = rms_norm(x, norm_g)
    p = jnp.einsum('nld,dp->nlp', h, w_in)

    cur = p[..., :SHIFT_W]
    prev = jnp.concatenate([prev_row[:, None, :].astype(cur.dtype), cur[:, :-1]], axis=1)
    sh = (cur + (prev - cur) * shift_mu).astype(f32)
    r = sh[..., :W_A]
    k = sh[..., W_A:2 * W_A]
    v = sh[..., 2 * W_A:3 * W_A]
    wd = sh[..., 3 * W_A:3 * W_A + LORA_W]
    ad = sh[..., 3 * W_A + LORA_W:]
    w_log = -jax.nn.softplus(-(w0.astype(f32) + jnp.tanh(wd) @ w2.astype(f32))) - 0.5
    decay = jnp.exp(-jnp.exp(w_log))
    a = jax.nn.sigmoid(a0.astype(f32) + ad @ a2.astype(f32))

    def heads_a(t):
        return t.reshape(n, L, H_A, HEAD_A)

    kk = heads_a(k * k_k.astype(f32))
    kk = kk / jnp.maximum(jnp.sqrt(jnp.sum(kk * kk, axis=-1, keepdims=True)), 1e-12)
    k = k * (1.0 + (a - 1.0) * k_a.astype(f32))
    r_h, k_h, v_h = heads_a(r), heads_a(k), heads_a(v)
    y_a, s_a_new = rwkv7_recurrence(r_h, heads_a(decay), k_h, v_h, kk, heads_a(a), s_a)
    mu = jnp.mean(y_a, axis=-1, keepdims=True)
    var = jnp.mean(jnp.square(y_a - mu), axis=-1, keepdims=True)
    y_a = ((y_a - mu) * lax.rsqrt(var + GN_EPS)).reshape(n, L, W_A) * ln_w.astype(f32) + ln_b.astype(f32)
    bonus = jnp.sum(r_h * k_h * r_k.astype(f32), axis=-1, keepdims=True) * v_h
    y_a = (y_a + bonus.reshape(n, L, W_A)) * jax.nn.silu(p[..., OFF_ZA:OFF_QB].astype(f32))

    def heads_b(t, d):
        return jnp.swapaxes(t.reshape(n, L, H_B, d), 1, 2)

    q_b = jax.nn.silu(p[..., OFF_QB:OFF_FB].astype(f32))
    f_b = p[..., OFF_FB:OFF_IB].astype(f32)
    i_b = p[..., OFF_IB:OFF_ZB].astype(f32)
    k_b = jnp.minimum((1.0 - lb) * jax.nn.sigmoid(-f_b), MAX_INPUT_GATE)
    log_g = jnp.log1p(-k_b)
    o_b, s_b_new = gla_chunkwise(heads_b(q_b, DK_B), heads_b(k_b, DK_B), heads_b(i_b, DV_B),
                                 heads_b(log_g, DK_B), s_b)
    o_b = rms_norm(jnp.swapaxes(o_b, 1, 2), hg_g).reshape(n, L, W_B)
    y_b = o_b * jax.nn.silu(p[..., OFF_ZB:OFF_GA].astype(f32))

    dt = x.dtype
    merged = (jax.nn.sigmoid(p[..., OFF_GA:OFF_GB].astype(f32))
              * jnp.einsum('nlw,wd->nld', y_a.astype(dt), proj_a)
              + jax.nn.sigmoid(p[..., OFF_GB:].astype(f32))
              * jnp.einsum('nlw,wd->nld', y_b.astype(dt), proj_b))
    y = x + jnp.einsum('nld,de->nle', merged.astype(dt), w_out)
    return y, cur[:, -1], s_a_new.astype(s_a.dtype), s_b_new.astype(s_b.dtype)


def setup_inputs(seed: int = 0) -> dict:
    key = jax.random.key(seed)
    ks = jax.random.split(key, 24)
    f32 = jnp.float32

    def nrm(k, shape, s):
        return jax.random.normal(k, shape, f32) * s

    return {
        "x_prompt": nrm(ks[0], (BATCH, SEQ, D_MODEL), 1.0),
        "x_sample": nrm(ks[1], (DEC_BATCH, DEC_SEQ, D_MODEL), 1.0),
        "state_rwkv": nrm(ks[2], (DEPTH, DEC_BATCH, H_A, HEAD_A, HEAD_A), 0.3),
        "state_hgrn": nrm(ks[3], (DEPTH, DEC_BATCH, H_B, DK_B, DV_B), 0.3),
        "state_shift": nrm(ks[4], (DEPTH, DEC_BATCH, SHIFT_W), 1.0),
        "norm_g": 1.0 + nrm(ks[5], (DEPTH, D_MODEL), 0.05),
        "w_in": nrm(ks[6], (DEPTH, D_MODEL, P_TOTAL), D_MODEL ** -0.5),
        "shift_mu": jax.random.uniform(ks[7], (DEPTH, SHIFT_W), f32),
        "rwkv_w0": -0.5 + nrm(ks[8], (DEPTH, W_A), 0.5),
        "rwkv_w2": nrm(ks[9], (DEPTH, LORA_W, W_A), 0.5 * LORA_W ** -0.5),
        "rwkv_a0": nrm(ks[10], (DEPTH, W_A), 0.1),
        "rwkv_a2": nrm(ks[11], (DEPTH, LORA_A, W_A), 0.5 * LORA_A ** -0.5),
        "rwkv_k_k": 0.85 + nrm(ks[12], (DEPTH, W_A), 0.05),
        "rwkv_k_a": 1.0 + nrm(ks[13], (DEPTH, W_A), 0.05),
        "rwkv_r_k": nrm(ks[14], (DEPTH, H_A, HEAD_A), 0.1),
        "rwkv_ln_w": 1.0 + nrm(ks[15], (DEPTH, W_A), 0.05),
        "rwkv_ln_b": nrm(ks[16], (DEPTH, W_A), 0.01),
        "hgrn_lb_logits": nrm(ks[17], (DEPTH, W_B), 0.1),
        "hgrn_norm_g": 1.0 + nrm(ks[18], (DEPTH, DV_B), 0.05),
        "proj_a": nrm(ks[19], (DEPTH, W_A, D_MODEL), W_A ** -0.5),
        "proj_b": nrm(ks[20], (DEPTH, W_B, D_MODEL), W_B ** -0.5),
        "w_out": nrm(ks[21], (DEPTH, D_MODEL, D_MODEL), D_MODEL ** -0.5),
        "final_norm_g": 1.0 + nrm(ks[22], (D_MODEL,), 0.05),
    }


def reference(x_prompt, x_sample, state_rwkv, state_hgrn, state_shift, norm_g, w_in, shift_mu,
              rwkv_w0, rwkv_w2, rwkv_a0, rwkv_a2, rwkv_k_k, rwkv_k_a, rwkv_r_k, rwkv_ln_w,
              rwkv_ln_b, hgrn_lb_logits, hgrn_norm_g, proj_a, proj_b, w_out, final_norm_g):
    probs = jax.nn.softmax(hgrn_lb_logits.astype(jnp.float32), axis=0)
    lbs = jnp.cumsum(probs, axis=0) - probs[0]
    nb = x_prompt.shape[0]
    zero_shift = jnp.zeros((nb, SHIFT_W), x_prompt.dtype)
    zero_a = jnp.zeros((nb, H_A, HEAD_A, HEAD_A), state_rwkv.dtype)
    zero_b = jnp.zeros((nb, H_B, DK_B, DV_B), state_hgrn.dtype)
    xp, xs = x_prompt, x_sample
    p_ra, p_hb, p_sh, s_ra, s_hb, s_sh = [], [], [], [], [], []
    for l in range(DEPTH):
        lw = (norm_g[l], w_in[l], shift_mu[l], rwkv_w0[l], rwkv_w2[l], rwkv_a0[l], rwkv_a2[l],
              rwkv_k_k[l], rwkv_k_a[l], rwkv_r_k[l], rwkv_ln_w[l], rwkv_ln_b[l], hgrn_norm_g[l],
              proj_a[l], proj_b[l], w_out[l])
        lw_head, lw_tail = lw[:12], lw[12:]
        xp, sh_p, a_p, b_p = mixer_layer(xp, zero_shift, zero_a, zero_b, lbs[l], *lw_head, *lw_tail)
        xs, sh_s, a_s, b_s = mixer_layer(xs, state_shift[l], state_rwkv[l], state_hgrn[l], lbs[l],
                                         *lw_head, *lw_tail)
        p_ra.append(a_p); p_hb.append(b_p); p_sh.append(sh_p)
        s_ra.append(a_s); s_hb.append(b_s); s_sh.append(sh_s)
    y_prompt = rms_norm(xp, final_norm_g)
    y_sample = rms_norm(xs, final_norm_g)
    return (y_prompt, y_sample, jnp.stack(p_ra), jnp.stack(p_hb), jnp.stack(p_sh),
            jnp.stack(s_ra), jnp.stack(s_hb), jnp.stack(s_sh))
```

```python
import contextlib
import os
KA = int(os.environ.get('KA', '99'))
KT = int(os.environ.get('KT', '99'))
import numpy as np
import concourse.bass as bass
import concourse.mybir as mybir
from concourse.bass_utils import run_bass_kernel_spmd

F32 = mybir.dt.float32
BF16 = mybir.dt.bfloat16
AF = mybir.ActivationFunctionType
ALU = mybir.AluOpType
AX = mybir.AxisListType


class Buf:
    def __init__(self, name, t):
        self.name = name
        self.t = t
        self.writers = []
        self.readers = []
        self.dsem = None
        self.dcum = 0


class KB:
    def __init__(self, nc):
        self.nc = nc
        self.es = contextlib.ExitStack()
        self.eng = {}
        for name, h in (("pe", nc.tensor), ("dve", nc.vector), ("act", nc.scalar),
                        ("pool", nc.gpsimd), ("sp", nc.sync)):
            sem = self.es.enter_context(nc.semaphore("prog_" + name))
            self.eng[name] = dict(h=h, sem=sem, count=0, waited={}, prog=[])
        self.dsems = []
        self.nbuf = 0
        self.label = ""
        self.labels = {n: [] for n in self.eng}

    def sbuf(self, name, shape, dtype):
        t = self.es.enter_context(self.nc.sbuf_tensor("s_" + name, list(shape), dtype))
        return Buf(name, t)

    def psum(self, name, shape, dtype):
        t = self.es.enter_context(self.nc.psum_tensor("p_" + name, list(shape), dtype))
        return Buf(name, t)

    def dram(self, name, ap):
        return Buf(name, ap)

    @staticmethod
    def hand_over(src_bufs, dst_bufs):
        for d in dst_bufs:
            for s_ in src_bufs:
                d.readers = d.readers + s_.writers + s_.readers

    def _waits(self, ename, reads, writes):
        E = self.eng[ename]
        deps = []
        for b in reads:
            deps += b.writers
        for b in writes:
            deps += b.writers + b.readers
        need = {}
        for (s, v, who) in deps:
            k = id(s)
            if k not in need or need[k][1] < v:
                need[k] = (s, v, who)
        for (s, v, who) in need.values():
            if who == ename:
                if ename == "pe":
                    continue
            if E["waited"].get(id(s), 0) >= v:
                continue
            E["waited"][id(s)] = v
            E["prog"].append(("wait", s, v))

    def op(self, ename, fn, reads=(), writes=()):
        E = self.eng[ename]
        self._waits(ename, reads, writes)
        E["count"] += 1
        idx = E["count"]
        self.labels[ename].append(self.label)
        E["prog"].append(("op", fn))
        ent = (E["sem"], idx, ename)
        for b in reads:
            if b in writes:
                continue
            b.readers.append(ent)
        for b in writes:
            b.writers = [ent]
            b.readers = []

    def dma(self, ename, sb, out_ap, in_ap, write=True, other=None, group=False, kw=None):
        E = self.eng[ename]
        qc = "sw" if ename == "pool" else "hw"
        if sb.dsem is None:
            sb.dsem = {}
        if qc not in sb.dsem:
            sem = self.es.enter_context(self.nc.semaphore("d%s_%s" % (qc, sb.name)))
            sb.dsem[qc] = [sem, 0]
            self.dsems.append(sb.dsem[qc])
        ds = sb.dsem[qc]
        reads, writes = ([], [sb]) if write else ([sb], [])
        if other is not None:
            (reads if write else writes).append(other)
        if group and write:
            saved = sb.writers
            sb.writers = [w for w in sb.writers if w[2] != "dma"]
            self._waits(ename, reads, writes)
            sb.writers = saved
        else:
            self._waits(ename, reads, writes)
        ds[1] += 16
        ent = (ds[0], ds[1], "dma")
        E["prog"].append(("dma", out_ap, in_ap, ds[0], kw or {}))
        for b in reads:
            b.readers.append(ent)
        for b in writes:
            if group and b is sb:
                b.writers = [w for w in b.writers if w[2] == "dma"] + [ent]
            else:
                b.writers = [ent]
            b.readers = []

    def finish(self):
        nc = self.nc
        E = self.eng["sp"]
        for ds in self.dsems:
            if E["waited"].get(id(ds[0]), 0) < ds[1]:
                E["prog"].append(("wait", ds[0], ds[1]))
        for name in ("pe", "dve", "act", "pool"):
            X = self.eng[name]
            if X["count"]:
                E["prog"].append(("wait", X["sem"], X["count"]))

        def replay(name):
            X = self.eng[name]

            def run(e):
                for item in X["prog"]:
                    if item[0] == "wait":
                        e.wait_ge(item[1], item[2])
                    elif item[0] == "op":
                        item[1](e).then_inc(X["sem"], 1)
                    else:
                        e.dma_start(out=item[1], in_=item[2], **item[4]).then_inc(item[3], 16)
            return run

        with nc.Block() as block:
            block.sync(replay("sp"))
            block.tensor(replay("pe"))
            block.vector(replay("dve"))
            block.scalar(replay("act"))
            block.gpsimd(replay("pool"))
        self.es.close()


class V:
    def __init__(self, buf, ap):
        self.buf = buf
        self.ap = ap

    def __getitem__(self, key):
        return V(self.buf, self.ap[key])

    def re(self, pat, **kw):
        return V(self.buf, self.ap.rearrange(pat, **kw))

    def bc(self, axis, shape):
        return V(self.buf, self.ap.unsqueeze(axis).to_broadcast(list(shape)))

    def bitcast(self, dt):
        return V(self.buf, self.ap.bitcast(dt))


def _bufs(*vs):
    out = []
    for v in vs:
        if isinstance(v, V) and v.buf not in out:
            out.append(v.buf)
    return out


def _ap(v):
    return v.ap if isinstance(v, V) else v


class Gen:
    def __init__(self, kb):
        self.kb = kb
        self._ev = 0

    def view(self, buf):
        return V(buf, buf.t[:])

    def mm(self, out, lhsT, rhs, start=True, stop=True):
        o, l, r = out.ap, lhsT.ap, rhs.ap
        self.kb.op("pe", lambda e: e.matmul(o, l, r, start=start, stop=stop),
                   _bufs(lhsT, rhs), _bufs(out))

    def tr(self, out, in_, ident):
        o, i, d = out.ap, in_.ap, ident.ap
        self.kb.op("pe", lambda e: e.transpose(o, i, d), _bufs(in_, ident), _bufs(out))

    def tt(self, eng, out, a, b, op):
        if eng == "pool":
            eng = "dve"
        if eng == "poolx":
            eng = "pool"
        o, x, y = out.ap, a.ap, b.ap
        self.kb.op(eng, lambda e: e.tensor_tensor(o, x, y, op), _bufs(a, b), _bufs(out))

    def ts(self, eng, out, a, s1, s2, op0, op1=None):
        if eng == "pool":
            eng = "dve"
        o, x, p1, p2 = out.ap, a.ap, _ap(s1), _ap(s2)
        if op1 is None:
            fn = lambda e: e.tensor_scalar(o, x, p1, None, op0)
        else:
            fn = lambda e: e.tensor_scalar(o, x, p1, p2, op0, op1)
        self.kb.op(eng, fn, _bufs(a, s1, s2), _bufs(out))

    def stt(self, out, a, s, b, op0, op1):
        o, x, p, y = out.ap, a.ap, _ap(s), b.ap
        self.kb.op("dve", lambda e: e.scalar_tensor_tensor(o, x, p, y, op0, op1),
                   _bufs(a, s, b), _bufs(out))

    def act(self, out, a, func, bias=0.0, scale=1.0, accum=None):
        o, x, bb, sc = out.ap, a.ap, _ap(bias), _ap(scale)
        if accum is None:
            fn = lambda e: e.activation(o, x, func, bias=bb, scale=sc)
            w = _bufs(out)
        else:
            ac = accum.ap
            fn = lambda e: e.activation(o, x, func, bias=bb, scale=sc, accum_out=ac)
            w = _bufs(out, accum)
        self.kb.op("act", fn, _bufs(a, bias, scale), w)

    def cp(self, eng, out, a):
        if eng == "act":
            return self.act(out, a, AF.Copy)
        if eng == "pool":
            eng = "dve"
        o, x = out.ap, a.ap
        self.kb.op(eng, lambda e: e.tensor_copy(o, x), _bufs(a), _bufs(out))

    def ev(self):
        self._ev ^= 1
        return "dve" if self._ev else "act"

    def recip(self, out, a):
        o, x = out.ap, a.ap
        self.kb.op("dve", lambda e: e.reciprocal(o, x), _bufs(a), _bufs(out))

    def scan(self, out, d0, d1, init, op0, op1):
        o, x, y = out.ap, d0.ap, d1.ap
        self.kb.op("dve", lambda e: e.tensor_tensor_scan(o, x, y, init, op0, op1),
                   _bufs(d0, d1), _bufs(out))

    def rsum(self, out, a):
        o, x = out.ap, a.ap
        self.kb.op("dve", lambda e: e.reduce_sum(o, x, AX.X), _bufs(a), _bufs(out))

    def memset(self, eng, out, val):
        o = out.ap
        self.kb.op(eng, lambda e: e.memset(o, val), [], _bufs(out))

    def dma(self, q, sb, dram_ap, load=True, other=None, group=False):
        if load:
            self.kb.dma(q, sb.buf, sb.ap, dram_ap, write=True, other=other, group=group)
        else:
            self.kb.dma(q, sb.buf, dram_ap, sb.ap, write=False, other=other)


D = 2048
KC = 16
W_A = 1024
SHIFT_W = 3200
OFF_ZA = 3200
OFF_QB = 4224
OFF_FB = 5248
OFF_IB = 6272
OFF_ZB = 7296
OFF_GA = 8320
OFF_GB = 10368
P_TOTAL = 12416
NSEQ = 16
DSEQ = 8
CDEC = 0.6065306597126334
NORM_EPS = 1e-6
GN_EPS = 64e-5
MAXK = 1.0 - 1e-6

PV = {}
_o = 0
for _n, _w in (("mu", 25), ("w0", 8), ("a0", 8), ("k_k", 8), ("k_a", 8), ("r_k", 8),
               ("ln_w", 8), ("ln_b", 8), ("hg_g", 1), ("lbl", 32), ("ng", 16)):
    PV[_n] = _o
    _o += _w
NPV = _o


def _seg_masks(seglen):
    s = np.arange(128)
    same = (s[:, None] // seglen) == (s[None, :] // seglen)
    ML = (same & (s[:, None] < s[None, :])).astype(np.float32)
    MI = (same & (s[:, None] <= s[None, :])).astype(np.float32)
    return ML, MI


def build_consts():
    fcols, bcols = {}, {}
    fparts, bparts = [], []

    def addf(name, arr):
        fcols[name] = (sum(a.shape[1] for a in fparts), arr.shape[1])
        fparts.append(arr.astype(np.float32))

    def addb(name, arr):
        bcols[name] = (sum(a.shape[1] for a in bparts), arr.shape[1])
        bparts.append(arr.astype(np.float32))

    I = np.eye(128, dtype=np.float32)
    addf("ident", I)
    t = np.arange(512)
    addf("sc128", np.broadcast_to((t[:256] % 128 != 0).astype(np.float32), (128, 256)))
    addf("sc32", np.broadcast_to((t[:256] % 32 != 0).astype(np.float32), (128, 256)))
    addf("sc8", np.broadcast_to((t[:128] % 8 != 0).astype(np.float32), (128, 128)))
    addb("ident", I)
    blk = np.arange(128) // 64
    addb("bo", (blk[:, None] == blk[None, :]).astype(np.float32))
    addb("ones", np.ones((128, 128), np.float32))
    addb("hm", (blk[:, None] == np.arange(2)[None, :]).astype(np.float32))
    addb("i2", np.concatenate([I, I], 1))
    for kind, seglen in (("P", 128), ("S", 8)):
        ML, MI = _seg_masks(seglen)
        addb("m4a_" + kind, np.concatenate([ML, ML, MI, MI], 1))
        addb("m4b_" + kind, np.concatenate([MI, MI, -ML, -ML], 1))
        addb("m2c_" + kind, np.concatenate([-ML.T, -ML.T], 1))
    for kind, seglen in (("P", 32), ("S", 8)):
        ML, MI = _seg_masks(seglen)
        addb("mih_" + kind, MI)
        nseg = 128 // seglen
        s = np.arange(128)
        addb("rm_" + kind, (s[:, None] // seglen == np.arange(nseg)[None, :]).astype(np.float32))
    addb("d16", np.broadcast_to(np.eye(16, dtype=np.float32).reshape(1, 256), (128, 256)))
    cF = np.ascontiguousarray(np.concatenate(fparts, 1))
    cB = np.ascontiguousarray(np.concatenate(bparts, 1))
    return cF, cB, fcols, bcols


class Blk:
    def __init__(self, kind, row0, g0, n):
        self.kind = kind
        self.row0 = row0
        self.g0 = g0
        self.n = n
        self.ntile = n // 128


def make_groups(npt, tiles_per_group):
    groups = []
    t = 0
    while t < npt:
        g, g0 = [], 0
        for _ in range(tiles_per_group // 2):
            if t >= npt:
                break
            g.append(Blk("P", t * 128, g0, 256))
            g0 += 256
            t += 2
        groups.append(g)
    groups.append([Blk("S", npt * 128, 0, 128)])
    return groups


def build(depth, npt, tiles_per_group=4, debug=False):
    NPTOK = npt * 128
    NTOK = NPTOK + 128
    groups = make_groups(npt, tiles_per_group)
    TG = max(sum(b.n for b in g) for g in groups)
    cF_np, cB_np, fcols, bcols = build_consts()
    NCF, NCB = cF_np.shape[1], cB_np.shape[1]

    nc = bass.Bass("TRN2", target_bir_lowering=False)

    def din(name, shape):
        return nc.dram_tensor(name, list(shape), F32, kind="ExternalInput").ap()

    def dout(name, shape):
        return nc.dram_tensor(name, list(shape), F32, kind="ExternalOutput").ap()

    xin = din("xin", [NTOK, D])
    st_rwkv = din("st_rwkv", [depth, NSEQ, 16, 64, 64])
    st_hgrn = din("st_hgrn", [depth, NSEQ, 8, 128, 128])
    st_shift = din("st_shift", [depth, NSEQ, SHIFT_W])
    w_in = din("w_in", [depth, D, P_TOTAL])
    proj_a = din("proj_a", [depth, W_A, D])
    proj_b = din("proj_b", [depth, W_A, D])
    w_out = din("w_out", [depth, D, D])
    w2a2_d = din("w2a2", [depth, 128, 1024])
    gbc_d = din("gbc", [128, D])
    pv_d = din("pv", [depth, 128, NPV])
    cF_d = din("cstF", [128, NCF])
    cB_d = din("cstB", [128, NCB])

    y = dout("y", [NTOK, D])
    o_p_rwkv = dout("o_p_rwkv", [depth, 16, 64, 64])
    o_p_hgrn = dout("o_p_hgrn", [depth, 8, 128, 128])
    o_p_shift = dout("o_p_shift", [depth, SHIFT_W])
    o_s_rwkv = dout("o_s_rwkv", [depth, NSEQ, 16, 64, 64])
    o_s_hgrn = dout("o_s_hgrn", [depth, NSEQ, 8, 128, 128])
    o_s_shift = dout("o_s_shift", [depth, NSEQ, SHIFT_W])

    kb = KB(nc)
    g = Gen(kb)
    SB = lambda name, shape, dt: g.view(kb.sbuf(name, shape, dt))

    cF = SB("cF", [128, NCF], F32)
    cB = SB("cB", [128, NCB], BF16)
    g.dma("sp", cF, cF_d)
    g.dma("pool", cB, cB_d)
    CF = lambda n: cF[:, fcols[n][0]:fcols[n][0] + fcols[n][1]]
    CB = lambda n: cB[:, bcols[n][0]:bcols[n][0] + bcols[n][1]]
    identF, identB = CF("ident"), CB("ident")
    BO, ONES = CB("bo"), CB("ones")

    pvt = SB("pvt", [128, NPV], F32)
    w2a2 = SB("w2a2", [128, 1024], BF16)
    hT = SB("hT", [128, KC, TG], BF16)
    yaT = SB("yaT", [128, 8, TG], BF16)
    ybT = SB("ybT", [128, 8, TG], BF16)
    mT = SB("mT", [128, KC, TG], BF16)
    lin = SB("lin", [128, TG], BF16)
    T32 = SB("T32", [128, 8, 128], F32)
    Tbd = SB("Tbd", [128, 8, 128], BF16)
    S32 = SB("S32", [128, 8, 128], F32)
    Sbf = SB("Sbf", [128, 8, 128], BF16)
    carry = SB("carry", [128, 25], F32)
    shS = SB("shS", [128, 25, NSEQ], F32)
    shO = SB("shO", [128, 25, NSEQ], F32)
    oml = SB("oml", [128, 8, depth], F32)
    Wslab = [SB("W%d" % i, [128, 8192], BF16) for i in range(2)]
    xt = [SB("xt%d" % i, [128, D], F32) for i in range(1)]
    hb = SB("hb", [128, D], BF16)
    junk = hb
    NF, NBF = 25, 9
    Fb = [SB("tf%d" % i, [128, 260], F32) if i not in (1, 2, 22, 24) else None for i in range(NF)]
    Fb[1] = SB("tf1", [128, 260], F32)
    Fb[2] = SB("tf2", [128, 260], F32)
    Bb = [SB("tb%d" % i, [128, 256], BF16) for i in range(NBF)]
    FbB = [V(Buf("fbB%d" % i, None), xt[0].ap[:, i * 260:(i + 1) * 260]) for i in range(7)]
    _hbf = hb.ap.bitcast(F32)
    FbB += [V(Buf("fbB%d" % (7 + i), None), _hbf[:, i * 260:(i + 1) * 260]) for i in range(3)]
    FbB += [Fb[1], Fb[2]]
    BbB = [SB("tbB%d" % i, [128, 256], BF16) for i in range(4)]
    _alias_src = [xt[0].buf, hb.buf]
    _alias_dst = [v.buf for v in FbB[0:10]]
    hwa = SB("hwa", [128, 16], F32)
    sm = {}

    def SM(name, shape, dt):
        if name not in sm:
            sm[name] = SB(name, shape, dt)
        return sm[name]

    ps = [g.view(kb.psum("ps%d" % i, [128, 512], F32)) for i in range(8)]
    cnt = {"w": 0, "x": 0}
    POOL_ALL = {"banks": ps, "k": 0}
    POOL_PREP = {"banks": ps[0:3], "k": 0}
    POOL_ST2 = {"banks": ps[3:5], "k": 0}
    POOL_B = {"banks": ps[6:8], "k": 0, "fixed": ps[5]}
    cur_pool = [POOL_ALL]

    def nps():
        p = cur_pool[0]
        p["k"] += 1
        return p["banks"][p["k"] % len(p["banks"])]

    def npsb():
        return nps().bitcast(BF16)

    def nW():
        cnt["w"] += 1
        return Wslab[cnt["w"] % 2]

    def nxt():
        cnt["x"] += 1
        return xt[0]

    pcol = lambda name, i: pvt[:, PV[name] + i:PV[name] + i + 1]

    ytile = [kb.dram("ytile%d" % i, None) for i in range(NTOK // 128)]

    def compute_lbs():
        g.dma("sp", pvt, pv_d[0])
        e = SM("lb_e", [128, 8, depth], F32)
        se = SM("lb_se", [128, 8], F32)
        lbl = pvt[:, PV["lbl"]:PV["lbl"] + 32].re("p (h l) -> p h l", l=4)[:, :, 0:depth]
        g.act(e, lbl, AF.Exp)
        g.rsum(se, e)
        g.recip(se, se)
        g.tt("dve", e, e, se.bc(2, [128, 8, depth]), ALU.mult)
        g.memset("dve", oml[:, :, 0:1], 1.0)
        for l in range(1, depth):
            g.tt("dve", oml[:, :, l:l + 1], oml[:, :, l - 1:l], e[:, :, l:l + 1], ALU.subtract)

    compute_lbs()

    def rms_rows(x, l_idx):
        ss = SM("n_ss", [128, 1], F32)
        rs = SM("n_rs", [128, 1], F32)
        g.memset("pool", ss, 0.0)
        g.act(junk, x, AF.Square, accum=ss)
        g.act(rs, ss, AF.Ln, bias=NORM_EPS, scale=1.0 / D)
        g.act(rs, rs, AF.Exp, scale=-0.5)
        return rs

    def phase_N(l, grp):
        for blk in grp:
            for ti in range(blk.ntile):
                row = blk.row0 + ti * 128
                x = nxt()
                if l == 0:
                    g.dma("sp", x, xin[row:row + 128, :])
                else:
                    g.dma("sp", x, y[row:row + 128, :], other=ytile[row // 128])
                import os
                KN = int(os.environ.get("KN", "9"))
                rs = rms_rows(x, l)
                if KN >= 2:
                    g.act(hb, x, AF.Identity, scale=rs)
                col = blk.g0 + ti * 128
                for r in range(2 if KN >= 3 else 0):
                    pb = npsb()
                    for k in range(8):
                        c = r * 8 + k
                        g.tr(pb[:, k * 128:(k + 1) * 128], hb[:, c * 128:(c + 1) * 128], identB)
                    ng = pvt[:, PV["ng"] + r * 8:PV["ng"] + r * 8 + 8]
                    if KN >= 4:
                        g.tt("dve", hT[:, r * 8:(r + 1) * 8, col:col + 128],
                             pb.re("p (k t) -> p k t", k=8), ng.bc(2, [128, 8, 128]), ALU.mult)

    def proj(Wv, j, blk, out_ps):
        for c in range(KC):
            g.mm(out_ps, Wv[:, j, c, :], hT[:, c, blk.g0:blk.g0 + blk.n],
                 start=(c == 0), stop=(c == KC - 1))

    def load_slab(l, cols, which=None):
        W = nW() if which is None else Wslab[which]
        Wv = W.re("p (j c n) -> p j c n", j=4, c=KC)
        for j, c0 in enumerate(cols):
            g.dma("pool", Wv[:, j], w_in[l, :, c0:c0 + 128].rearrange("(c p) n -> p c n", p=128),
                  group=True)
        return Wv

    def shift(ps_v, pS, dtmp, cb, blk, out):
        n = blk.n
        mu = pcol("mu", cb)
        if blk.kind == "P":
            g.cp("pool", pS[:, 0:1], carry[:, cb:cb + 1])
            g.cp("act", pS[:, 1:n + 1], ps_v)
            g.cp("pool", carry[:, cb:cb + 1], pS[:, n:n + 1])
            g.tt("pool", dtmp[:, 0:n], pS[:, 0:n], pS[:, 1:n + 1], ALU.subtract)
            g.stt(out, dtmp[:, 0:n], mu, pS[:, 1:n + 1], ALU.mult, ALU.add)
        else:
            p3 = pS[:, 0:NSEQ * 9].re("p (j i) -> p j i", i=9)
            d3 = dtmp[:, 0:n].re("p (j i) -> p j i", i=DSEQ)
            o3 = out.re("p (j i) -> p j i", i=DSEQ)
            g.cp("pool", p3[:, :, 0], shS[:, cb, :])
            g.cp("act", p3[:, :, 1:9], ps_v.re("p (j i) -> p j i", i=DSEQ))
            g.cp("pool", shO[:, cb, :], p3[:, :, 8])
            g.tt("pool", d3, p3[:, :, 0:8], p3[:, :, 1:9], ALU.subtract)
            g.stt(o3, d3, mu, p3[:, :, 1:9], ALU.mult, ALU.add)

    def phase_L(l, grp):
        Wv = load_slab(l, [3 * W_A])
        for blk in grp:
            n = blk.n
            pp = nps()
            proj(Wv, 0, blk, pp[:, 0:n])
            sh = Fb[2][:, 0:n]
            shift(pp[:, 0:n], Fb[0], Fb[1], 24, blk, sh)
            g.act(lin[0:64, blk.g0:blk.g0 + n], sh[0:64], AF.Tanh)
            g.cp("pool", lin[64:128, blk.g0:blk.g0 + n], sh[64:128])

    d16 = CB("d16").re("p (a b) -> p a b", a=16)
    kapb = [SB("a_kapb%d" % i, [128, 256], BF16) for i in range(2)]
    rtb = [SB("a_rtb%d" % i, [128, 256], BF16) for i in range(2)]
    gCb = [SB("a_gC%d" % i, [128, 16], F32) for i in range(2)]
    bonb = [SB("a_bon%d" % i, [128, 260], F32) for i in range(2)]
    szb = [SB("a_sz%d" % i, [128, 260], F32) for i in range(2)]
    sbd = [SB("a_sbd%d" % i, [128, 4, 128], F32) for i in range(2)]
    for s_ in sbd:
        g.memset("pool", s_, 0.0)
    Zbd = SB("a_zbd", [128, NSEQ, 128], BF16)
    HS = [slice(0, 64), slice(64, 128)]

    def load_rwkv_state(l, hp):
        for q in range(4):
            S_ = sbd[q % 2]
            for h in range(2):
                g.dma("sp", S_[HS[h], :, HS[h]],
                      st_rwkv[l, 4 * q:4 * q + 4, 2 * hp + h].rearrange("s v k -> v s k"),
                      group=True)
            pT = nps()
            for jj in range(4):
                g.tr(pT[:, jj * 128:(jj + 1) * 128], S_[:, jj], identF)
            p3 = pT.re("p (a b) -> p a b", a=4)
            g.cp(g.ev(), Zbd[:, 4 * q:4 * q + 4], p3)

    def rwkv_stage1(l, hp, blk, par):
        kind = blk.kind
        P = kind == "P"
        nlev = 6 if P else 2
        fl = lambda v: v.re("p a b -> p (a b)")
        hm = CB("hm").bc(2, [128, 2, 128])
        T = []
        for ti in range(blk.ntile):
            c0 = ti * 128
            csl = slice(c0, c0 + 128)
            kt, kh, bt, bh, vb = (Bb[i][:, csl] for i in (2, 3, 4, 5, 8))
            kap, rt = kapb[par][:, csl], rtb[par][:, csl]
            sfx = "_%d_%d" % (ti, par)
            pb = npsb()
            g.tr(pb[:, 0:128], vb, identB)
            yield
            g.tr(pb[:, 128:256], kh, identB)
            yield
            g.tr(pb[:, 256:384], bh, identB)
            yield
            tm = SM("a_tm" + sfx, [128, 3, 128], BF16)
            g.cp("act", tm, pb[:, 0:384].re("p (a b) -> p a b", a=3))
            yield
            bd = {}
            for nm, src in (("kap", kap), ("rt", rt), ("bt", bt)):
                d_ = SM("a_bd_" + nm, [128, 2, 128], BF16)
                g.tt("poolx", d_, src.bc(1, [128, 2, 128]), hm, ALU.mult)
                yield
                bd[nm] = d_.re("p a b -> p (a b)")
            pA, pB, pC = nps(), nps(), nps()
            g.mm(pA[:, 0:256], kt, bd["kap"])
            yield
            g.mm(pA[:, 256:512], kt, bd["rt"])
            yield
            g.mm(pB[:, 0:256], bt, bd["rt"])
            yield
            g.mm(pB[:, 256:512], bt, bd["kap"])
            yield
            g.mm(pC[:, 0:256], kap, bd["bt"])
            yield
            AKB = SM("a_akb" + sfx, [128, 4, 128], BF16)
            BX = SM("a_bx" + sfx, [128, 4, 128], BF16)
            X0 = SM("a_x0_%d" % ti, [128, 2, 128], BF16)
            g.tt("dve", fl(AKB), pA, CB("m4a_" + kind), ALU.mult)
            yield
            g.tt("dve", fl(BX), pB, CB("m4b_" + kind), ALU.mult)
            yield
            g.tt("dve", fl(X0), pC[:, 0:256], CB("m2c_" + kind), ALU.mult)
            yield
            PT = SM("a_pt" + sfx, [128, 2, 128], BF16)
            g.tt("pool", PT, BX[:, 2:4], CB("i2").re("p (a b) -> p a b", a=2), ALU.add)
            yield
            T.append(dict(csl=csl, tm=tm, AKB=AKB, BX=BX, PT=PT, kap=kap, rt=rt,
                          X=[(X0, BX[:, 2:4])], sfx="_%d" % ti))
            yield
        for step in range(1, nlev + 2):
            for t in T:
                Xp, XTp = t["X"][step - 1] if step - 1 < len(t["X"]) else (None, None)
                PT = t["PT"]
                pX = nps()
                need_sq = step <= nlev
                need_T = step < nlev
                if need_sq:
                    for h in range(2):
                        g.mm(pX[:, h * 128:(h + 1) * 128], XTp[:, h], Xp[:, h])
                        yield
                    if need_T:
                        for h in range(2):
                            g.mm(pX[:, (2 + h) * 128:(3 + h) * 128], Xp[:, h], XTp[:, h])
                            yield
                if step >= 2:
                    pP = nps()
                    for h in range(2):
                        g.mm(pP[:, h * 128:(h + 1) * 128], Xp[:, h], PT[:, h])
                        yield
                if need_sq:
                    Xn = SM("a_xn%d%s" % (step % 2, t["sfx"]), [128, 4, 128], BF16)
                    w = 512 if need_T else 256
                    g.cp("act", fl(Xn)[:, 0:w], pX[:, 0:w])
                    yield
                    t["X"].append((Xn[:, 0:2], Xn[:, 2:4]))
                if step >= 2:
                    g.tt("dve", fl(PT), fl(PT), pP[:, 0:256], ALU.add)
            yield
        return T

    def rwkv_tile_stage2(l, hp, blk, ti, t):
        kind = blk.kind
        P = kind == "P"
        csl = t["csl"]
        tm, AKB, BX, PT, kap, rt = t["tm"], t["AKB"], t["BX"], t["PT"], t["kap"], t["rt"]
        Vt, khT, bhT = tm[:, 0], tm[:, 1], tm[:, 2]
        gC = t["gC"]
        pR, pY = nps(), nps()

        def head_groups(pO, opnd, name, tails):
            oms = []
            if not P:
                o4 = opnd.re("p (j i) -> p j i", i=DSEQ).bc(1, [128, 4, NSEQ, DSEQ])
                for q in range(4):
                    om = SM("%s%d" % (name, q), [128, 4, 128], BF16)
                    g.tt("poolx", om.re("p a (j i) -> p a j i", i=DSEQ), o4,
                         d16[:, 4 * q:4 * q + 4, :].bc(3, [128, 4, NSEQ, DSEQ]), ALU.mult)
                    oms.append(om)
            for h in range(2):
                if P:
                    g.mm(pO[:, HS[h]], opnd, Tbd[:, hp][:, HS[h]], start=True, stop=False)
                else:
                    for q in range(4):
                        for jj in range(4):
                            g.mm(pO[:, HS[h]], oms[q][:, jj], Zbd[:, 4 * q + jj][:, HS[h]],
                                 start=(q == 0 and jj == 0), stop=False)
                tl = tails(h)
                for i, (a_, b_) in enumerate(tl):
                    g.mm(pO[:, HS[h]], a_, b_, start=False, stop=(i == len(tl) - 1))

        head_groups(pR, kap, "a_msk", lambda h: [(AKB[:, h], Vt[:, HS[h]])])
        yield
        R1 = SM("a_r1", [128, 128], BF16)
        g.cp("act", R1, pR[:, 0:128])
        yield
        pU = nps()
        for h in range(2):
            g.mm(pU[:, HS[h]], PT[:, h], R1[:, HS[h]])
            yield
        Un = SM("a_un", [128, 128], BF16)
        g.act(Un, pU[:, 0:128], AF.Identity, scale=-1.0)
        yield
        head_groups(pY, rt, "a_msk", lambda h: [(AKB[:, 2 + h], Vt[:, HS[h]]),
                                                (BX[:, h], Un[:, HS[h]])])
        yield
        Ysb = SM("a_ysb", [128, 128], F32)
        g.cp("act", Ysb, pY[:, 0:128])
        yield
        if P:
            pZ = nps()
            g.mm(pZ[:, 0:128], khT, Vt, start=True, stop=False)
            yield
            g.mm(pZ[:, 0:128], bhT, Un, start=False, stop=True)
            yield
            Wz = SM("a_wz", [128, 128], F32)
            g.tt("dve", Wz, pZ[:, 0:128], BO, ALU.mult)
            yield
            g.stt(T32[:, hp], T32[:, hp], gC[:, ti:ti + 1], Wz, ALU.mult, ALU.add)
            yield
            g.cp("act", Tbd[:, hp], T32[:, hp])
            yield
        else:
            rmS = CB("rm_S")
            for q in range(4):
                khTm = SM("a_khTm", [128, 4, 128], BF16)
                bhTm = SM("a_bhTm", [128, 4, 128], BF16)
                rmq = rmS[:, 4 * q:4 * q + 4].bc(2, [128, 4, 128])
                g.tt("poolx", khTm, khT.bc(1, [128, 4, 128]), rmq, ALU.mult)
                g.tt("poolx", bhTm, bhT.bc(1, [128, 4, 128]), rmq, ALU.mult)
                pZ = nps()
                for jj in range(4):
                    g.mm(pZ[:, jj * 128:(jj + 1) * 128], khTm[:, jj], Vt, start=True, stop=False)
                    g.mm(pZ[:, jj * 128:(jj + 1) * 128], bhTm[:, jj], Un, start=False, stop=True)
                Wq = SM("q4f_0", [128, 4, 128], F32)
                g.tt("dve", Wq, pZ.re("p (a b) -> p a b", a=4), BO.bc(1, [128, 4, 128]), ALU.mult)
                S_ = sbd[q % 2]
                for h in range(2):
                    g.dma("sp", S_[HS[h], :, HS[h]],
                          st_rwkv[l, 4 * q:4 * q + 4, 2 * hp + h].rearrange("s v k -> v s k"),
                          group=True)
                pT0 = nps()
                for jj in range(4):
                    g.tr(pT0[:, jj * 128:(jj + 1) * 128], S_[:, jj], identF)
                tq = SM("q4f_1", [128, 4, 128], F32)
                g.tt("dve", tq, pT0.re("p (a b) -> p a b", a=4),
                     gC[:, 4 * q:4 * q + 4].bc(2, [128, 4, 128]), ALU.mult)
                g.tt("pool", tq, tq, Wq, ALU.add)
                pT = nps()
                for jj in range(4):
                    g.tr(pT[:, jj * 128:(jj + 1) * 128], tq[:, jj], identF)
                So = SM("q4f_2", [128, 4, 128], F32)
                g.cp("act", So, pT.re("p (a b) -> p a b", a=4))
                for h in range(2):
                    g.dma("sp", So[HS[h], :, HS[h]],
                          o_s_rwkv[l, 4 * q:4 * q + 4, 2 * hp + h].rearrange("s v k -> v s k"),
                          load=False)
        s1 = SM("a_s1", [128, 2], F32)
        s2 = SM("a_s2", [128, 2], F32)
        mean = SM("a_mean", [128, 2], F32)
        msq = SM("a_msq", [128, 2], F32)
        var = SM("a_var", [128, 2], F32)
        nmr = SM("a_nmr", [128, 2], F32)
        g.rsum(s1, Ysb.re("p (h v) -> p h v", h=2))
        yield
        g.memset("pool", s2, 0.0)
        yield
        for h in range(2):
            g.act(SM("a_jk", [128, 64], BF16), Ysb[:, HS[h]], AF.Square, accum=s2[:, h:h + 1])
            yield
        g.ts("dve", mean, s1, 1.0 / 64, None, ALU.mult)
        yield
        g.tt("dve", msq, mean, mean, ALU.mult)
        yield
        g.stt(var, s2, 1.0 / 64, msq, ALU.mult, ALU.subtract)
        yield
        g.act(var, var, AF.Ln, bias=GN_EPS)
        yield
        g.act(var, var, AF.Exp, scale=-0.5)
        yield
        g.stt(nmr, mean, -1.0, var, ALU.mult, ALU.mult)
        yield
        yn = SM("a_yn", [128, 128], BF16)
        for h in range(2):
            g.act(yn[:, HS[h]], Ysb[:, HS[h]], AF.Identity, bias=nmr[:, h:h + 1], scale=var[:, h:h + 1])
            yield
        pb2 = npsb()
        g.tr(pb2[:, 0:128], yn, identB)
        yield
        g.ts("dve", Fb[23][:, csl], pb2[:, 0:128], pcol("ln_w", hp), pcol("ln_b", hp), ALU.mult, ALU.add)
        yield

    def prep_A(l, hp, Wv, blk, par):
        n = blk.n
        P = blk.kind == "P"
        C = 128 if P else DSEQ
        nch = n // C
        gsl = slice(blk.g0, blk.g0 + n)
        F = lambda i: Fb[i][:, 0:n]
        B = lambda i: Bb[i][:, 0:n]
        wsl = slice(hp * 128, (hp + 1) * 128)
        pp = nps()
        proj(Wv, 0, blk, pp[:, 0:n])
        yield
        shift(pp[:, 0:n], Fb[0], Fb[3], hp, blk, F(4))
        yield
        pp = nps()
        proj(Wv, 1, blk, pp[:, 0:n])
        yield
        shift(pp[:, 0:n], Fb[0], Fb[3], 8 + hp, blk, F(5))
        yield
        pp = nps()
        proj(Wv, 2, blk, pp[:, 0:n])
        yield
        shift(pp[:, 0:n], Fb[0], Fb[3], 16 + hp, blk, F(6))
        yield
        pp = nps()
        proj(Wv, 3, blk, pp[:, 0:n])
        yield
        g.act(szb[par][:, 0:n], pp[:, 0:n], AF.Tanh, scale=0.5)
        yield
        g.stt(szb[par][:, 0:n], szb[par][:, 0:n], 1.0, pp[:, 0:n], ALU.add, ALU.mult)
        yield
        pp = nps()
        g.mm(pp[:, 0:n], w2a2[0:64, wsl], lin[0:64, gsl])
        yield
        g.act(F(8), pp[:, 0:n], AF.Tanh, bias=hwa[:, hp:hp + 1], scale=0.5)
        yield
        g.ts("dve", F(8), F(8), 0.5, 0.5, ALU.mult, ALU.add)
        yield
        scm = (CF("sc128") if P else CF("sc8"))[:, 0:n]
        g.scan(F(9), scm, F(8), 0.0, ALU.mult, ALU.add)
        yield
        g.tt("pool", F(10), F(9), F(8), ALU.subtract)
        yield
        g.act(F(11), F(9), AF.Exp, scale=-CDEC)
        yield
        g.act(F(12), F(9), AF.Exp, scale=CDEC)
        yield
        g.act(F(13), F(10), AF.Exp, scale=-CDEC)
        yield
        cs3 = F(9).re("p (c i) -> p c i", i=C)
        g.act(gCb[par][:, 0:nch], cs3[:, :, C - 1], AF.Exp, scale=-CDEC)
        yield
        g.tt("pool", F(10).re("p (c i) -> p c i", i=C), cs3[:, :, C - 1].bc(2, [128, nch, C]),
             cs3, ALU.subtract)
        yield
        g.act(F(14), F(10), AF.Exp, scale=-CDEC)
        yield
        pp = nps()
        g.mm(pp[:, 0:n], w2a2[64:128, wsl], lin[64:128, gsl])
        yield
        g.act(F(15), pp[:, 0:n], AF.Tanh, bias=hwa[:, 8 + hp:9 + hp], scale=0.5)
        yield
        g.ts("dve", F(15), F(15), 0.5, 0.5, ALU.mult, ALU.add)
        yield
        g.ts("pool", F(16), F(5), pcol("k_k", hp), None, ALU.mult)
        yield
        g.tt("pool", B(0), F(16), F(16), ALU.mult)
        yield
        pp = nps()
        g.mm(pp[:, 0:n], BO, B(0))
        yield
        g.act(F(17), pp[:, 0:n], AF.Ln, bias=1e-24)
        yield
        g.act(F(17), F(17), AF.Exp, scale=-0.5)
        yield
        g.tt("pool", F(18), F(16), F(17), ALU.mult)
        yield
        g.ts("dve", F(19), F(15), 1.0, pcol("k_a", hp), ALU.subtract, ALU.mult)
        yield
        g.stt(F(20), F(19), 1.0, F(5), ALU.add, ALU.mult)
        yield
        g.tt("pool", F(21), F(18), F(15), ALU.mult)
        yield
        g.stt(B(1), F(4), pcol("r_k", hp), F(20), ALU.mult, ALU.mult)
        yield
        pp = nps()
        g.mm(pp[:, 0:n], BO, B(1))
        yield
        g.tt("dve", bonb[par][:, 0:n], pp[:, 0:n], F(6), ALU.mult)
        yield
        g.tt("poolx", B(2), F(20), F(12), ALU.mult)
        yield
        g.tt("poolx", B(3), F(20), F(14), ALU.mult)
        yield
        g.tt("poolx", B(4), F(21), F(12), ALU.mult)
        yield
        g.tt("poolx", B(5), F(21), F(14), ALU.mult)
        yield
        g.tt("pool", kapb[par][:, 0:n], F(18), F(13), ALU.mult)
        yield
        g.tt("pool", rtb[par][:, 0:n], F(4), F(11), ALU.mult)
        yield
        g.cp("pool", B(8), F(6))
        yield
        T = yield from rwkv_stage1(l, hp, blk, par)
        for t in T:
            t["gC"] = gCb[par]
        return dict(T=T, par=par, hp=hp, blk=blk)

    def stage2_A(l, ctx):
        hp, blk, par = ctx["hp"], ctx["blk"], ctx["par"]
        n = blk.n
        gsl = slice(blk.g0, blk.g0 + n)
        if blk.kind == "S":
            load_rwkv_state(l, hp)
            yield
        for ti, t in enumerate(ctx["T"]):
            yield from rwkv_tile_stage2(l, hp, blk, ti, t)
            yield
        ynT = Fb[23][:, 0:n]
        g.tt("pool", ynT, ynT, bonb[par][:, 0:n], ALU.add)
        yield
        g.stt(yaT[:, hp, gsl], ynT, 0.5, szb[par][:, 0:n], ALU.mult, ALU.mult)
        yield

    def out_rwkv_prompt(l, hp):
        pT = nps()
        g.tr(pT[:, 0:128], T32[:, hp], identF)
        So = SM("a_sop", [128, 128], F32)
        g.cp("act", So, pT[:, 0:128])
        for h in range(2):
            g.dma("sp", So[HS[h], HS[h]], o_p_rwkv[l, 2 * hp + h], load=False)

    def tasks_A(l, grp, last_group):
        items = [(hp, bi) for hp in range(8) for bi in range(len(grp))]
        queue, done = [], [0]

        def prep_task():
            Wv = None
            for i, (hp, bi) in enumerate(items):
                while i - done[0] >= 2:
                    yield
                if bi == 0:
                    Wv = load_slab(l, [hp * 128, W_A + hp * 128, 2 * W_A + hp * 128,
                                       OFF_ZA + hp * 128], 0)
                ctx = yield from prep_A(l, hp, Wv, grp[bi], i % 2)
                queue.append(ctx)
                yield

        def stage2_task():
            for i, (hp, bi) in enumerate(items):
                while not queue:
                    yield
                ctx = queue.pop(0)
                yield from stage2_A(l, ctx)
                done[0] += 1
                if bi == len(grp) - 1 and last_group:
                    out_rwkv_prompt(l, hp)
                yield

        return [(prep_task(), POOL_PREP), (stage2_task(), POOL_ST2)]

    def tasks_B(l, grp, last_group):
        def b_task():
            for hbi in range(8):
                yield from task_B(l, grp, hbi, last_group)
        return [(b_task(), POOL_B)]

    def run_tasks(tasks):
        tasks = list(tasks)
        while tasks:
            for t in list(tasks):
                cur_pool[0] = t[1]
                try:
                    next(t[0])
                except StopIteration:
                    tasks.remove(t)
        cur_pool[0] = POOL_ALL

    def hgrn_tile(l, hbi, blk, ti, vtok):
        kind = blk.kind
        P = kind == "P"
        C = 32 if P else DSEQ
        nseg = 128 // C
        c0 = ti * 128
        csl = slice(c0, c0 + 128)
        qt, kt, kh = (BbB[i][:, csl] for i in (0, 1, 2))
        vt = vtok[:, ti]
        gl = FbB[11]
        pA = nps()
        g.mm(pA[:, 0:128], kt, qt)
        yield
        attm = SM("b_attm", [128, 128], BF16)
        g.tt("dve", attm, pA[:, 0:128], CB("mih_" + kind), ALU.mult)
        yield
        pb = npsb()
        g.tr(pb[:, 0:128], kh, identB)
        yield
        khs = SM("b_khs", [128, 128], BF16)
        g.cp("act", khs, pb[:, 0:128])
        yield
        khb = khs.bc(1, [128, 4, 128])
        pO = cur_pool[0]["fixed"]
        if P:
            khm = SM("b_khm", [128, 4, 128], BF16)
            g.tt("poolx", khm, khb, CB("rm_P").bc(2, [128, 4, 128]), ALU.mult)
            yield
            for i in range(nseg):
                ss = slice(C * i, C * i + C)
                g.mm(pO[:, ss], vt, attm[:, ss], start=True, stop=False)
                yield
                g.mm(pO[:, ss], Sbf[:, hbi], qt[:, ss], start=False, stop=True)
                yield
                pS_ = nps()
                g.mm(pS_[:, 0:128], khm[:, i], vt)
                yield
                g.stt(S32[:, hbi], S32[:, hbi], gl[:, ti * nseg + i:ti * nseg + i + 1],
                      pS_[:, 0:128], ALU.mult, ALU.add)
                yield
                g.cp("act", Sbf[:, hbi], S32[:, hbi])
                yield
        else:
            rmS = CB("rm_S")
            for q in range(4):
                Sin = SM("q4f_%d" % (q % 2), [128, 4, 128], F32)
                g.dma("sp", Sin, st_hgrn[l, 4 * q:4 * q + 4, hbi].rearrange("s d e -> d s e"))
                Sib = SM("b_sib", [128, 4, 128], BF16)
                g.cp("act", Sib, Sin)
                khm = SM("b_khm", [128, 4, 128], BF16)
                g.tt("poolx", khm, khb, rmS[:, 4 * q:4 * q + 4].bc(2, [128, 4, 128]), ALU.mult)
                pS_ = nps()
                for jj in range(4):
                    j = 4 * q + jj
                    ss = slice(C * j, C * j + C)
                    g.mm(pO[:, ss], vt, attm[:, ss], start=True, stop=False)
                    g.mm(pO[:, ss], Sib[:, jj], qt[:, ss], start=False, stop=True)
                    g.mm(pS_[:, jj * 128:(jj + 1) * 128], khm[:, jj], vt)
                So = SM("q4f_%d" % (2 + q % 2), [128, 4, 128], F32)
                g.tt("pool", So, Sin, gl[:, 4 * q:4 * q + 4].bc(2, [128, 4, 128]), ALU.mult)
                g.tt("dve", So, So, pS_.re("p (a b) -> p a b", a=4), ALU.add)
                g.dma("sp", So, o_s_hgrn[l, 4 * q:4 * q + 4, hbi].rearrange("s d e -> d s e"),
                      load=False)
                yield
        g.cp("act", FbB[9][:, csl], pO[:, 0:128])
        yield

    def phase_B_block(l, hbi, Wv, blk):
        n = blk.n
        P = blk.kind == "P"
        C = 32 if P else DSEQ
        nch = n // C
        gsl = slice(blk.g0, blk.g0 + n)
        F = lambda i: FbB[i][:, 0:n]
        B = lambda i: BbB[i][:, 0:n]
        pp = nps()
        proj(Wv, 0, blk, pp[:, 0:n])
        yield
        g.act(F(0), pp[:, 0:n], AF.Tanh, scale=0.5)
        yield
        g.stt(F(0), F(0), 1.0, pp[:, 0:n], ALU.add, ALU.mult)
        yield
        pp = nps()
        proj(Wv, 1, blk, pp[:, 0:n])
        yield
        g.act(F(1), pp[:, 0:n], AF.Tanh, scale=0.5)
        yield
        g.ts("dve", F(1), F(1), -0.5, 0.5, ALU.mult, ALU.add)
        yield
        g.ts("dve", F(1), F(1), oml[:, hbi, l:l + 1], MAXK, ALU.mult, ALU.min)
        yield
        g.act(F(2), F(1), AF.Ln, bias=1.0, scale=-1.0)
        yield
        scm = (CF("sc32") if P else CF("sc8"))[:, 0:n]
        g.scan(F(3), scm, F(2), 0.0, ALU.mult, ALU.add)
        yield
        cs3 = F(3).re("p (c i) -> p c i", i=C)
        g.tt("pool", F(4).re("p (c i) -> p c i", i=C), cs3[:, :, C - 1].bc(2, [128, nch, C]),
             cs3, ALU.subtract)
        yield
        g.act(F(5), F(3), AF.Exp)
        yield
        g.act(F(6), F(3), AF.Exp, scale=-1.0)
        yield
        g.act(F(7), F(4), AF.Exp)
        yield
        g.act(FbB[11][:, 0:nch], cs3[:, :, C - 1], AF.Exp)
        yield
        g.stt(B(0), F(0), 0.5, F(5), ALU.mult, ALU.mult)
        yield
        g.tt("pool", B(1), F(1), F(6), ALU.mult)
        yield
        g.tt("pool", B(2), F(1), F(7), ALU.mult)
        yield
        pp = nps()
        proj(Wv, 3, blk, pp[:, 0:n])
        yield
        g.act(F(8), pp[:, 0:n], AF.Tanh, scale=0.5)
        yield
        g.stt(F(8), F(8), 1.0, pp[:, 0:n], ALU.add, ALU.mult)
        yield
        vtok = SM("b_vtok", [128, 2, 128], BF16)
        for ti in range(blk.ntile):
            pv_ = nps()
            col = blk.g0 + ti * 128
            for c in range(KC):
                g.mm(pv_[:, 0:128], hT[:, c, col:col + 128], Wv[:, 2, c, :],
                     start=(c == 0), stop=(c == KC - 1))
                yield
            g.cp(g.ev(), vtok[:, ti], pv_[:, 0:128])
        yield
        for ti in range(blk.ntile):
            yield from hgrn_tile(l, hbi, blk, ti, vtok)
            yield
        g.tt("pool", B(3), F(9), F(9), ALU.mult)
        yield
        pp = nps()
        g.mm(pp[:, 0:n], ONES, B(3))
        yield
        g.act(F(10), pp[:, 0:n], AF.Ln, bias=NORM_EPS, scale=1.0 / 128)
        yield
        g.act(F(10), F(10), AF.Exp, scale=-0.5)
        yield
        g.stt(F(9), F(9), pcol("hg_g", 0), F(10), ALU.mult, ALU.mult)
        yield
        g.stt(ybT[:, hbi, gsl], F(9), 0.5, F(8), ALU.mult, ALU.mult)
        yield

    def task_B(l, grp, hbi, last_group):
        cols = [OFF_QB + hbi * 128, OFF_FB + hbi * 128, OFF_IB + hbi * 128, OFF_ZB + hbi * 128]
        Wv = load_slab(l, cols, 1)
        for blk in grp:
            yield from phase_B_block(l, hbi, Wv, blk)
            yield
        if last_group:
            g.dma("sp", S32[:, hbi], o_p_hgrn[l, hbi], load=False)

    def phase_M(l, grp):
        def load_m(dc):
            W = nW()
            Wga = W[:, 0:2048].re("p (c n) -> p c n", c=KC)
            Wgb = W[:, 2048:4096].re("p (c n) -> p c n", c=KC)
            Wpa = W[:, 4096:5120].re("p (c n) -> p c n", c=8)
            Wpb = W[:, 5120:6144].re("p (c n) -> p c n", c=8)
            dsl = slice(dc * 128, dc * 128 + 128)
            g.dma("pool", Wga, w_in[l, :, OFF_GA + dc * 128:OFF_GA + dc * 128 + 128]
                  .rearrange("(c p) n -> p c n", p=128), group=True)
            g.dma("pool", Wgb, w_in[l, :, OFF_GB + dc * 128:OFF_GB + dc * 128 + 128]
                  .rearrange("(c p) n -> p c n", p=128), group=True)
            g.dma("pool", Wpa, proj_a[l, :, dsl].rearrange("(c p) n -> p c n", p=128), group=True)
            g.dma("pool", Wpb, proj_b[l, :, dsl].rearrange("(c p) n -> p c n", p=128), group=True)
            return Wga, Wgb, Wpa, Wpb

        tot_ = sum(b.n for b in grp)
        spans = [(o_, min(512, tot_ - o_)) for o_ in range(0, tot_, 512)]
        Wn = load_m(0)
        for dc in range(KC):
            Wga, Wgb, Wpa, Wpb = Wn
            if dc < KC - 1:
                Wn = load_m(dc + 1)
            for (g0_, n) in spans:
                gsl = slice(g0_, g0_ + n)
                F = lambda i: SM("q4f_%d" % i, [128, 4, 128], F32).re("p a b -> p (a b)")[:, 0:n]
                p1 = nps()
                for c in range(KC):
                    g.mm(p1[:, 0:n], Wga[:, c], hT[:, c, gsl], start=(c == 0), stop=(c == KC - 1))
                g.act(F(0), p1[:, 0:n], AF.Sigmoid)
                p2 = nps()
                for c in range(8):
                    g.mm(p2[:, 0:n], Wpa[:, c], yaT[:, c, gsl], start=(c == 0), stop=(c == 7))
                g.tt("dve", F(1), p2[:, 0:n], F(0), ALU.mult)
                p3 = nps()
                for c in range(KC):
                    g.mm(p3[:, 0:n], Wgb[:, c], hT[:, c, gsl], start=(c == 0), stop=(c == KC - 1))
                g.act(F(2), p3[:, 0:n], AF.Sigmoid)
                p4 = nps()
                for c in range(8):
                    g.mm(p4[:, 0:n], Wpb[:, c], ybT[:, c, gsl], start=(c == 0), stop=(c == 7))
                g.tt("dve", F(3), p4[:, 0:n], F(2), ALU.mult)
                g.tt("pool", mT[:, dc, gsl], F(1), F(3), ALU.add)

        def load_o(eb):
            W = nW()
            Wo = W.re("p (c n) -> p c n", c=KC)
            g.dma("pool", Wo, w_out[l, :, eb * 512:(eb + 1) * 512].rearrange("(c p) n -> p c n", p=128))
            return Wo

        Wn = load_o(0)
        k = 0
        for eb in range(4):
            Wo = Wn
            if eb < 3:
                Wn = load_o(eb + 1)
            esl = slice(eb * 512, (eb + 1) * 512)
            for blk in grp:
                for ti in range(blk.ntile):
                    row = blk.row0 + ti * 128
                    col = blk.g0 + ti * 128
                    po = nps()
                    for dc in range(KC):
                        g.mm(po, mT[:, dc, col:col + 128], Wo[:, dc], start=(dc == 0), stop=(dc == KC - 1))
                    k += 1
                    xs_ = SM("q4f_%d" % (k % 2), [128, 4, 128], F32).re("p a b -> p (a b)")
                    yt = ytile[row // 128]
                    if l == 0:
                        g.dma("sp", xs_, xin[row:row + 128, esl])
                    else:
                        g.dma("sp", xs_, y[row:row + 128, esl], other=yt)
                    g.tt("dve", xs_, xs_, po, ALU.add)
                    g.dma("sp", xs_, y[row:row + 128, esl], load=False, other=yt)

    def load_shift_state(l):
        for r in range(7):
            nb = min(4, 25 - 4 * r)
            st = SM("q4f_3", [128, 4, 128], F32).re("p a b -> p (a b)")[0:NSEQ, 0:nb * 128]
            g.dma("sp", st, st_shift[l, :, r * 512:r * 512 + nb * 128])
            pT = nps()
            for k_ in range(nb):
                g.tr(pT[:, k_ * NSEQ:(k_ + 1) * NSEQ], st[:, k_ * 128:(k_ + 1) * 128],
                     identF[0:NSEQ, 0:NSEQ])
            g.cp("act", shS[:, 4 * r:4 * r + nb, :].re("p b s -> p (b s)"), pT[:, 0:nb * NSEQ])

    def store_shift_state(l):
        for r in range(7):
            nb = min(4, 25 - 4 * r)
            pT = nps()
            for k_ in range(nb):
                b = 4 * r + k_
                g.tr(pT[0:NSEQ, k_ * 128:(k_ + 1) * 128], shO[:, b, :], identF)
            so = SM("q4f_%d" % (2 + r % 2), [128, 4, 128], F32).re("p a b -> p (a b)")[0:NSEQ, 0:nb * 128]
            g.cp("act", so, pT[0:NSEQ, 0:nb * 128])
            g.dma("sp", so, o_s_shift[l, :, r * 512:r * 512 + nb * 128], load=False)
        pT = nps()
        g.tr(pT[0:25, 0:128], carry, identF)
        sp_ = SM("sh_outp", [25, 128], F32)
        g.cp("act", sp_, pT[0:25, 0:128])
        g.dma("sp", sp_, o_p_shift[l].rearrange("(b p) -> b p", p=128), load=False)

    for l in range(depth):
        if l > 0:
            g.dma("sp", pvt, pv_d[l])
        g.dma("pool", w2a2, w2a2_d[l])
        g.ts("dve", hwa, pvt[:, PV["w0"]:PV["w0"] + 16], 0.5, None, ALU.mult)
        load_shift_state(l)
        for t_ in (T32, S32, carry):
            g.memset("pool", t_, 0.0)
        for t_ in (Tbd, Sbf):
            g.memset("pool", t_, 0.0)
        for gi, grp in enumerate(groups):
            last = gi == len(groups) - 1
            last_p = gi == len(groups) - 2
            import os
            stop = int(os.environ.get("KSTOP", "99"))
            kb.label = "N"
            if stop >= 1:
                phase_N(l, grp)
            kb.label = "L"
            if stop >= 2:
                phase_L(l, grp)
            kb.label = "AB"
            KB.hand_over(_alias_src, _alias_dst)
            run_tasks(tasks_A(l, grp, last_p) + tasks_B(l, grp, last_p))
            KB.hand_over(_alias_dst, _alias_src)
            kb.label = "M"
            if stop >= 5:
                phase_M(l, grp)
        if stop >= 6:
            store_shift_state(l)

    gbc = Wslab[0][:, 0:2 * D].bitcast(F32)
    g.dma("sp", gbc, gbc_d)
    for ti in range(NTOK // 128 if stop >= 7 else 0):
        x = nxt()
        yt = ytile[ti]
        g.dma("sp", x, y[ti * 128:(ti + 1) * 128, :], other=yt)
        rs = rms_rows(x, depth)
        g.stt(x, x, rs, gbc, ALU.mult, ALU.mult)
        g.dma("sp", x, y[ti * 128:(ti + 1) * 128, :], load=False, other=yt)

    kb.finish()
    nc._kb_labels = kb.labels
    return nc, (cF_np, cB_np)


_CACHE = {}


def _f32(a):
    return np.ascontiguousarray(np.asarray(a, dtype=np.float32))


def kernel(x_prompt, x_sample, state_rwkv, state_hgrn, state_shift, norm_g, w_in, shift_mu,
           rwkv_w0, rwkv_w2, rwkv_a0, rwkv_a2, rwkv_k_k, rwkv_k_a, rwkv_r_k, rwkv_ln_w,
           rwkv_ln_b, hgrn_lb_logits, hgrn_norm_g, proj_a, proj_b, w_out, final_norm_g,
           _tiles_per_group=4):
    x_prompt, x_sample = _f32(x_prompt), _f32(x_sample)
    state_rwkv, state_hgrn, state_shift = _f32(state_rwkv), _f32(state_hgrn), _f32(state_shift)
    depth = int(np.asarray(norm_g).shape[0])
    nb, seq = x_prompt.shape[0], x_prompt.shape[1]
    npt = seq // 128
    ncores = x_sample.shape[0] // NSEQ
    key = (depth, npt, _tiles_per_group)
    if key not in _CACHE:
        _CACHE[key] = build(depth, npt, _tiles_per_group)
    nc, (cF, cB) = _CACHE[key]

    w_in, proj_a, proj_b, w_out = _f32(w_in), _f32(proj_a), _f32(proj_b), _f32(w_out)
    w2a2 = np.ascontiguousarray(np.concatenate([_f32(rwkv_w2), _f32(rwkv_a2)], axis=1))
    gbc = np.ascontiguousarray(np.broadcast_to(_f32(final_norm_g)[None, :], (128, D)))
    pv = np.zeros((depth, 128, NPV), np.float32)
    lb = _f32(hgrn_lb_logits).reshape(depth, 8, 128).transpose(2, 1, 0)
    lbp = np.zeros((128, 8, 4), np.float32)
    lbp[:, :, :depth] = lb
    for l in range(depth):
        def put(name, vec, ncol):
            pv[l, :, PV[name]:PV[name] + ncol] = _f32(vec).reshape(ncol, 128).T
        put("mu", shift_mu[l], 25)
        put("w0", rwkv_w0[l], 8)
        put("a0", rwkv_a0[l], 8)
        put("k_k", rwkv_k_k[l], 8)
        put("k_a", rwkv_k_a[l], 8)
        put("r_k", np.asarray(rwkv_r_k[l]).reshape(-1), 8)
        put("ln_w", rwkv_ln_w[l], 8)
        put("ln_b", rwkv_ln_b[l], 8)
        put("hg_g", hgrn_norm_g[l], 1)
        put("ng", norm_g[l], 16)
        pv[l, :, PV["lbl"]:PV["lbl"] + 32] = lbp.reshape(128, 32)

    in_maps = []
    for c in range(ncores):
        sl = slice(NSEQ * c, NSEQ * (c + 1))
        xin = np.concatenate([x_prompt[c % nb], x_sample[sl].reshape(NSEQ * DSEQ, D)], axis=0)
        in_maps.append({
            "xin": np.ascontiguousarray(xin),
            "st_rwkv": np.ascontiguousarray(state_rwkv[:, sl]),
            "st_hgrn": np.ascontiguousarray(state_hgrn[:, sl]),
            "st_shift": np.ascontiguousarray(state_shift[:, sl]),
            "w_in": w_in, "proj_a": proj_a, "proj_b": proj_b, "w_out": w_out,
            "w2a2": w2a2, "gbc": gbc, "pv": pv, "cstF": cF, "cstB": cB,
        })
    res = run_bass_kernel_spmd(nc, in_maps, core_ids=list(range(ncores))).results
    NP = npt * 128
    y_prompt = np.stack([res[b]["y"][:NP] for b in range(nb)], axis=0)
    y_sample = np.concatenate([res[c]["y"][NP:].reshape(NSEQ, DSEQ, D) for c in range(ncores)], axis=0)
    p_ra = np.stack([res[b]["o_p_rwkv"] for b in range(nb)], axis=1)
    p_hb = np.stack([res[b]["o_p_hgrn"] for b in range(nb)], axis=1)
    p_sh = np.stack([res[b]["o_p_shift"] for b in range(nb)], axis=1)
    s_ra = np.concatenate([res[c]["o_s_rwkv"] for c in range(ncores)], axis=1)
    s_hb = np.concatenate([res[c]["o_s_hgrn"] for c in range(ncores)], axis=1)
    s_sh = np.concatenate([res[c]["o_s_shift"] for c in range(ncores)], axis=1)
    return tuple(np.ascontiguousarray(a, dtype=np.float32)
                 for a in (y_prompt, y_sample, p_ra, p_hb, p_sh, s_ra, s_hb, s_sh))
```

```python
import contextlib
import os
KA = int(os.environ.get('KA', '99'))
KT = int(os.environ.get('KT', '99'))
import numpy as np
import concourse.bass as bass
import concourse.mybir as mybir
from concourse.bass_utils import run_bass_kernel_spmd

F32 = mybir.dt.float32
BF16 = mybir.dt.bfloat16
AF = mybir.ActivationFunctionType
ALU = mybir.AluOpType
AX = mybir.AxisListType


class Buf:
    def __init__(self, name, t):
        self.name = name
        self.t = t
        self.writers = []
        self.readers = []
        self.dsem = None
        self.dcum = 0


class KB:
    def __init__(self, nc):
        self.nc = nc
        self.es = contextlib.ExitStack()
        self.eng = {}
        for name, h in (("pe", nc.tensor), ("dve", nc.vector), ("act", nc.scalar),
                        ("pool", nc.gpsimd), ("sp", nc.sync)):
            sem = self.es.enter_context(nc.semaphore("prog_" + name))
            self.eng[name] = dict(h=h, sem=sem, count=0, waited={}, prog=[])
        self.dsems = []
        self.nbuf = 0
        self.label = ""
        self.labels = {n: [] for n in self.eng}

    def sbuf(self, name, shape, dtype):
        t = self.es.enter_context(self.nc.sbuf_tensor("s_" + name, list(shape), dtype))
        return Buf(name, t)

    def psum(self, name, shape, dtype):
        t = self.es.enter_context(self.nc.psum_tensor("p_" + name, list(shape), dtype))
        return Buf(name, t)

    def dram(self, name, ap):
        return Buf(name, ap)

    @staticmethod
    def hand_over(src_bufs, dst_bufs):
        for d in dst_bufs:
            for s_ in src_bufs:
                d.readers = d.readers + s_.writers + s_.readers

    def _waits(self, ename, reads, writes):
        E = self.eng[ename]
        deps = []
        for b in reads:
            deps += b.writers
        for b in writes:
            deps += b.writers + b.readers
        need = {}
        for (s, v, who) in deps:
            k = id(s)
            if k not in need or need[k][1] < v:
                need[k] = (s, v, who)
        for (s, v, who) in need.values():
            if who == ename:
                if ename == "pe":
                    continue
            if E["waited"].get(id(s), 0) >= v:
                continue
            E["waited"][id(s)] = v
            E["prog"].append(("wait", s, v))

    def op(self, ename, fn, reads=(), writes=()):
        E = self.eng[ename]
        self._waits(ename, reads, writes)
        E["count"] += 1
        idx = E["count"]
        self.labels[ename].append(self.label)
        E["prog"].append(("op", fn))
        ent = (E["sem"], idx, ename)
        for b in reads:
            if b in writes:
                continue
            b.readers.append(ent)
        for b in writes:
            b.writers = [ent]
            b.readers = []

    def dma(self, ename, sb, out_ap, in_ap, write=True, other=None, group=False, kw=None):
        E = self.eng[ename]
        qc = "sw" if ename == "pool" else "hw"
        if sb.dsem is None:
            sb.dsem = {}
        if qc not in sb.dsem:
            sem = self.es.enter_context(self.nc.semaphore("d%s_%s" % (qc, sb.name)))
            sb.dsem[qc] = [sem, 0]
            self.dsems.append(sb.dsem[qc])
        ds = sb.dsem[qc]
        reads, writes = ([], [sb]) if write else ([sb], [])
        if other is not None:
            (reads if write else writes).append(other)
        if group and write:
            saved = sb.writers
            sb.writers = [w for w in sb.writers if w[2] != "dma"]
            self._waits(ename, reads, writes)
            sb.writers = saved
        else:
            self._waits(ename, reads, writes)
        ds[1] += 16
        ent = (ds[0], ds[1], "dma")
        E["prog"].append(("dma", out_ap, in_ap, ds[0], kw or {}))
        for b in reads:
            b.readers.append(ent)
        for b in writes:
            if group and b is sb:
                b.writers = [w for w in b.writers if w[2] == "dma"] + [ent]
            else:
                b.writers = [ent]
            b.readers = []

    def finish(self):
        nc = self.nc
        E = self.eng["sp"]
        for ds in self.dsems:
            if E["waited"].get(id(ds[0]), 0) < ds[1]:
                E["prog"].append(("wait", ds[0], ds[1]))
        for name in ("pe", "dve", "act", "pool"):
            X = self.eng[name]
            if X["count"]:
                E["prog"].append(("wait", X["sem"], X["count"]))

        def replay(name):
            X = self.eng[name]

            def run(e):
                for item in X["prog"]:
                    if item[0] == "wait":
                        e.wait_ge(item[1], item[2])
                    elif item[0] == "op":
                        item[1](e).then_inc(X["sem"], 1)
                    else:
                        e.dma_start(out=item[1], in_=item[2], **item[4]).then_inc(item[3], 16)
            return run

        with nc.Block() as block:
            block.sync(replay("sp"))
            block.tensor(replay("pe"))
            block.vector(replay("dve"))
            block.scalar(replay("act"))
            block.gpsimd(replay("pool"))
        self.es.close()


class V:
    def __init__(self, buf, ap):
        self.buf = buf
        self.ap = ap

    def __getitem__(self, key):
        return V(self.buf, self.ap[key])

    def re(self, pat, **kw):
        return V(self.buf, self.ap.rearrange(pat, **kw))

    def bc(self, axis, shape):
        return V(self.buf, self.ap.unsqueeze(axis).to_broadcast(list(shape)))

    def bitcast(self, dt):
        return V(self.buf, self.ap.bitcast(dt))


def _bufs(*vs):
    out = []
    for v in vs:
        if isinstance(v, V) and v.buf not in out:
            out.append(v.buf)
    return out


def _ap(v):
    return v.ap if isinstance(v, V) else v


class Gen:
    def __init__(self, kb):
        self.kb = kb
        self._ev = 0

    def view(self, buf):
        return V(buf, buf.t[:])

    def mm(self, out, lhsT, rhs, start=True, stop=True):
        o, l, r = out.ap, lhsT.ap, rhs.ap
        self.kb.op("pe", lambda e: e.matmul(o, l, r, start=start, stop=stop),
                   _bufs(lhsT, rhs), _bufs(out))

    def tr(self, out, in_, ident):
        o, i, d = out.ap, in_.ap, ident.ap
        self.kb.op("pe", lambda e: e.transpose(o, i, d), _bufs(in_, ident), _bufs(out))

    def tt(self, eng, out, a, b, op):
        if eng == "pool":
            eng = "dve"
        if eng == "poolx":
            eng = "dve"
        o, x, y = out.ap, a.ap, b.ap
        self.kb.op(eng, lambda e: e.tensor_tensor(o, x, y, op), _bufs(a, b), _bufs(out))

    def ts(self, eng, out, a, s1, s2, op0, op1=None):
        if eng == "pool":
            eng = "dve"
        o, x, p1, p2 = out.ap, a.ap, _ap(s1), _ap(s2)
        if op1 is None:
            fn = lambda e: e.tensor_scalar(o, x, p1, None, op0)
        else:
            fn = lambda e: e.tensor_scalar(o, x, p1, p2, op0, op1)
        self.kb.op(eng, fn, _bufs(a, s1, s2), _bufs(out))

    def stt(self, out, a, s, b, op0, op1):
        o, x, p, y = out.ap, a.ap, _ap(s), b.ap
        self.kb.op("dve", lambda e: e.scalar_tensor_tensor(o, x, p, y, op0, op1),
                   _bufs(a, s, b), _bufs(out))

    def act(self, out, a, func, bias=0.0, scale=1.0, accum=None):
        o, x, bb, sc = out.ap, a.ap, _ap(bias), _ap(scale)
        if accum is None:
            fn = lambda e: e.activation(o, x, func, bias=bb, scale=sc)
            w = _bufs(out)
        else:
            ac = accum.ap
            fn = lambda e: e.activation(o, x, func, bias=bb, scale=sc, accum_out=ac)
            w = _bufs(out, accum)
        self.kb.op("act", fn, _bufs(a, bias, scale), w)

    def cp(self, eng, out, a):
        if eng == "act":
            return self.act(out, a, AF.Copy)
        if eng == "pool":
            eng = "dve"
        o, x = out.ap, a.ap
        self.kb.op(eng, lambda e: e.tensor_copy(o, x), _bufs(a), _bufs(out))

    def ev(self):
        self._ev ^= 1
        return "dve" if self._ev else "act"

    def recip(self, out, a):
        o, x = out.ap, a.ap
        self.kb.op("dve", lambda e: e.reciprocal(o, x), _bufs(a), _bufs(out))

    def scan(self, out, d0, d1, init, op0, op1):
        o, x, y = out.ap, d0.ap, d1.ap
        self.kb.op("dve", lambda e: e.tensor_tensor_scan(o, x, y, init, op0, op1),
                   _bufs(d0, d1), _bufs(out))

    def rsum(self, out, a):
        o, x = out.ap, a.ap
        self.kb.op("dve", lambda e: e.reduce_sum(o, x, AX.X), _bufs(a), _bufs(out))

    def memset(self, eng, out, val):
        o = out.ap
        self.kb.op(eng, lambda e: e.memset(o, val), [], _bufs(out))

    def dma(self, q, sb, dram_ap, load=True, other=None, group=False):
        if load:
            self.kb.dma(q, sb.buf, sb.ap, dram_ap, write=True, other=other, group=group)
        else:
            self.kb.dma(q, sb.buf, dram_ap, sb.ap, write=False, other=other)


D = 2048
KC = 16
W_A = 1024
SHIFT_W = 3200
OFF_ZA = 3200
OFF_QB = 4224
OFF_FB = 5248
OFF_IB = 6272
OFF_ZB = 7296
OFF_GA = 8320
OFF_GB = 10368
P_TOTAL = 12416
NSEQ = 16
DSEQ = 8
CDEC = 0.6065306597126334
NORM_EPS = 1e-6
GN_EPS = 64e-5
MAXK = 1.0 - 1e-6

PV = {}
_o = 0
for _n, _w in (("mu", 25), ("w0", 8), ("a0", 8), ("k_k", 8), ("k_a", 8), ("r_k", 8),
               ("ln_w", 8), ("ln_b", 8), ("hg_g", 1), ("lbl", 32), ("ng", 16)):
    PV[_n] = _o
    _o += _w
NPV = _o


def _seg_masks(seglen):
    s = np.arange(128)
    same = (s[:, None] // seglen) == (s[None, :] // seglen)
    ML = (same & (s[:, None] < s[None, :])).astype(np.float32)
    MI = (same & (s[:, None] <= s[None, :])).astype(np.float32)
    return ML, MI


def build_consts():
    fcols, bcols = {}, {}
    fparts, bparts = [], []

    def addf(name, arr):
        fcols[name] = (sum(a.shape[1] for a in fparts), arr.shape[1])
        fparts.append(arr.astype(np.float32))

    def addb(name, arr):
        bcols[name] = (sum(a.shape[1] for a in bparts), arr.shape[1])
        bparts.append(arr.astype(np.float32))

    I = np.eye(128, dtype=np.float32)
    addf("ident", I)
    t = np.arange(512)
    addf("sc128", np.broadcast_to((t[:256] % 128 != 0).astype(np.float32), (128, 256)))
    addf("sc32", np.broadcast_to((t[:256] % 32 != 0).astype(np.float32), (128, 256)))
    addf("sc8", np.broadcast_to((t[:128] % 8 != 0).astype(np.float32), (128, 128)))
    addb("ident", I)
    blk = np.arange(128) // 64
    addb("bo", (blk[:, None] == blk[None, :]).astype(np.float32))
    addb("ones", np.ones((128, 128), np.float32))
    addb("hm", (blk[:, None] == np.arange(2)[None, :]).astype(np.float32))
    addb("i2", np.concatenate([I, I], 1))
    for kind, seglen in (("P", 128), ("S", 8)):
        ML, MI = _seg_masks(seglen)
        addb("m4a_" + kind, np.concatenate([ML, ML, MI, MI], 1))
        addb("m4b_" + kind, np.concatenate([MI, MI, -ML, -ML], 1))
        addb("m2c_" + kind, np.concatenate([-ML.T, -ML.T], 1))
    for kind, seglen in (("P", 32), ("S", 8)):
        ML, MI = _seg_masks(seglen)
        addb("mih_" + kind, MI)
        nseg = 128 // seglen
        s = np.arange(128)
        addb("rm_" + kind, (s[:, None] // seglen == np.arange(nseg)[None, :]).astype(np.float32))
    addb("d16", np.broadcast_to(np.eye(16, dtype=np.float32).reshape(1, 256), (128, 256)))
    cF = np.ascontiguousarray(np.concatenate(fparts, 1))
    cB = np.ascontiguousarray(np.concatenate(bparts, 1))
    return cF, cB, fcols, bcols


class Blk:
    def __init__(self, kind, row0, g0, n):
        self.kind = kind
        self.row0 = row0
        self.g0 = g0
        self.n = n
        self.ntile = n // 128


def make_groups(npt, tiles_per_group):
    groups = []
    t = 0
    while t < npt:
        g, g0 = [], 0
        for _ in range(tiles_per_group // 2):
            if t >= npt:
                break
            g.append(Blk("P", t * 128, g0, 256))
            g0 += 256
            t += 2
        groups.append(g)
    groups.append([Blk("S", npt * 128, 0, 128)])
    return groups


def build(depth, npt, tiles_per_group=4, debug=False):
    NPTOK = npt * 128
    NTOK = NPTOK + 128
    groups = make_groups(npt, tiles_per_group)
    TG = max(sum(b.n for b in g) for g in groups)
    cF_np, cB_np, fcols, bcols = build_consts()
    NCF, NCB = cF_np.shape[1], cB_np.shape[1]

    nc = bass.Bass("TRN2", target_bir_lowering=False)

    def din(name, shape):
        return nc.dram_tensor(name, list(shape), F32, kind="ExternalInput").ap()

    def dout(name, shape):
        return nc.dram_tensor(name, list(shape), F32, kind="ExternalOutput").ap()

    xin = din("xin", [NTOK, D])
    st_rwkv = din("st_rwkv", [depth, NSEQ, 16, 64, 64])
    st_hgrn = din("st_hgrn", [depth, NSEQ, 8, 128, 128])
    st_shift = din("st_shift", [depth, NSEQ, SHIFT_W])
    w_in = din("w_in", [depth, D, P_TOTAL])
    proj_a = din("proj_a", [depth, W_A, D])
    proj_b = din("proj_b", [depth, W_A, D])
    w_out = din("w_out", [depth, D, D])
    w2a2_d = din("w2a2", [depth, 128, 1024])
    gbc_d = din("gbc", [128, D])
    pv_d = din("pv", [depth, 128, NPV])
    cF_d = din("cstF", [128, NCF])
    cB_d = din("cstB", [128, NCB])

    y = dout("y", [NTOK, D])
    o_p_rwkv = dout("o_p_rwkv", [depth, 16, 64, 64])
    o_p_hgrn = dout("o_p_hgrn", [depth, 8, 128, 128])
    o_p_shift = dout("o_p_shift", [depth, SHIFT_W])
    o_s_rwkv = dout("o_s_rwkv", [depth, NSEQ, 16, 64, 64])
    o_s_hgrn = dout("o_s_hgrn", [depth, NSEQ, 8, 128, 128])
    o_s_shift = dout("o_s_shift", [depth, NSEQ, SHIFT_W])

    kb = KB(nc)
    g = Gen(kb)
    SB = lambda name, shape, dt: g.view(kb.sbuf(name, shape, dt))

    cF = SB("cF", [128, NCF], F32)
    cB = SB("cB", [128, NCB], BF16)
    g.dma("sp", cF, cF_d)
    g.dma("pool", cB, cB_d)
    CF = lambda n: cF[:, fcols[n][0]:fcols[n][0] + fcols[n][1]]
    CB = lambda n: cB[:, bcols[n][0]:bcols[n][0] + bcols[n][1]]
    identF, identB = CF("ident"), CB("ident")
    BO, ONES = CB("bo"), CB("ones")

    pvt = SB("pvt", [128, NPV], F32)
    w2a2 = SB("w2a2", [128, 1024], BF16)
    hT = SB("hT", [128, KC, TG], BF16)
    yaT = SB("yaT", [128, 8, TG], BF16)
    ybT = SB("ybT", [128, 8, TG], BF16)
    mT = SB("mT", [128, KC, TG], BF16)
    lin = SB("lin", [128, TG], BF16)
    T32 = SB("T32", [128, 8, 128], F32)
    Tbd = SB("Tbd", [128, 8, 128], BF16)
    S32 = SB("S32", [128, 8, 128], F32)
    Sbf = SB("Sbf", [128, 8, 128], BF16)
    carry = SB("carry", [128, 25], F32)
    shS = SB("shS", [128, 25, NSEQ], F32)
    shO = SB("shO", [128, 25, NSEQ], F32)
    oml = SB("oml", [128, 8, depth], F32)
    Wslab = [SB("W%d" % i, [128, 8192], BF16) for i in range(2)]
    xt = [SB("xt%d" % i, [128, D], F32) for i in range(1)]
    hb = SB("hb", [128, D], BF16)
    junk = hb
    NF, NBF = 25, 9
    Fb = [SB("tf%d" % i, [128, 260], F32) if i not in (1, 2, 22, 24) else None for i in range(NF)]
    Fb[1] = SB("tf1", [128, 260], F32)
    Fb[2] = SB("tf2", [128, 260], F32)
    Bb = [SB("tb%d" % i, [128, 256], BF16) for i in range(NBF)]
    FbB = [V(Buf("fbB%d" % i, None), xt[0].ap[:, i * 260:(i + 1) * 260]) for i in range(7)]
    _hbf = hb.ap.bitcast(F32)
    FbB += [V(Buf("fbB%d" % (7 + i), None), _hbf[:, i * 260:(i + 1) * 260]) for i in range(3)]
    FbB += [Fb[1], Fb[2]]
    BbB = [SB("tbB%d" % i, [128, 256], BF16) for i in range(4)]
    _alias_src = [xt[0].buf, hb.buf]
    _alias_dst = [v.buf for v in FbB[0:10]]
    hwa = SB("hwa", [128, 16], F32)
    sm = {}

    def SM(name, shape, dt):
        if name not in sm:
            sm[name] = SB(name, shape, dt)
        return sm[name]

    ps = [g.view(kb.psum("ps%d" % i, [128, 512], F32)) for i in range(8)]
    cnt = {"w": 0, "x": 0}
    POOL_ALL = {"banks": ps, "k": 0}
    POOL_PREP = {"banks": ps[0:3], "k": 0}
    POOL_ST2 = {"banks": ps[3:5], "k": 0}
    POOL_B = {"banks": ps[6:8], "k": 0, "fixed": ps[5]}
    cur_pool = [POOL_ALL]

    def nps():
        p = cur_pool[0]
        p["k"] += 1
        return p["banks"][p["k"] % len(p["banks"])]

    def npsb():
        return nps().bitcast(BF16)

    def nW():
        cnt["w"] += 1
        return Wslab[cnt["w"] % 2]

    def nxt():
        cnt["x"] += 1
        return xt[0]

    pcol = lambda name, i: pvt[:, PV[name] + i:PV[name] + i + 1]

    ytile = [kb.dram("ytile%d" % i, None) for i in range(NTOK // 128)]

    def compute_lbs():
        g.dma("sp", pvt, pv_d[0])
        e = SM("lb_e", [128, 8, depth], F32)
        se = SM("lb_se", [128, 8], F32)
        lbl = pvt[:, PV["lbl"]:PV["lbl"] + 32].re("p (h l) -> p h l", l=4)[:, :, 0:depth]
        g.act(e, lbl, AF.Exp)
        g.rsum(se, e)
        g.recip(se, se)
        g.tt("dve", e, e, se.bc(2, [128, 8, depth]), ALU.mult)
        g.memset("dve", oml[:, :, 0:1], 1.0)
        for l in range(1, depth):
            g.tt("dve", oml[:, :, l:l + 1], oml[:, :, l - 1:l], e[:, :, l:l + 1], ALU.subtract)

    compute_lbs()

    def rms_rows(x, l_idx):
        ss = SM("n_ss", [128, 1], F32)
        rs = SM("n_rs", [128, 1], F32)
        g.memset("pool", ss, 0.0)
        g.act(junk, x, AF.Square, accum=ss)
        g.act(rs, ss, AF.Ln, bias=NORM_EPS, scale=1.0 / D)
        g.act(rs, rs, AF.Exp, scale=-0.5)
        return rs

    def phase_N(l, grp):
        for blk in grp:
            for ti in range(blk.ntile):
                row = blk.row0 + ti * 128
                x = nxt()
                if l == 0:
                    g.dma("sp", x, xin[row:row + 128, :])
                else:
                    g.dma("sp", x, y[row:row + 128, :], other=ytile[row // 128])
                import os
                KN = int(os.environ.get("KN", "9"))
                rs = rms_rows(x, l)
                if KN >= 2:
                    g.act(hb, x, AF.Identity, scale=rs)
                col = blk.g0 + ti * 128
                for r in range(2 if KN >= 3 else 0):
                    pb = npsb()
                    for k in range(8):
                        c = r * 8 + k
                        g.tr(pb[:, k * 128:(k + 1) * 128], hb[:, c * 128:(c + 1) * 128], identB)
                    ng = pvt[:, PV["ng"] + r * 8:PV["ng"] + r * 8 + 8]
                    if KN >= 4:
                        g.tt("dve", hT[:, r * 8:(r + 1) * 8, col:col + 128],
                             pb.re("p (k t) -> p k t", k=8), ng.bc(2, [128, 8, 128]), ALU.mult)

    def proj(Wv, j, blk, out_ps):
        for c in range(KC):
            g.mm(out_ps, Wv[:, j, c, :], hT[:, c, blk.g0:blk.g0 + blk.n],
                 start=(c == 0), stop=(c == KC - 1))

    def load_slab(l, cols, which=None):
        W = nW() if which is None else Wslab[which]
        Wv = W.re("p (j c n) -> p j c n", j=4, c=KC)
        for j, c0 in enumerate(cols):
            g.dma("pool", Wv[:, j], w_in[l, :, c0:c0 + 128].rearrange("(c p) n -> p c n", p=128),
                  group=True)
        return Wv

    def shift(ps_v, pS, dtmp, cb, blk, out):
        n = blk.n
        mu = pcol("mu", cb)
        if blk.kind == "P":
            g.cp("pool", pS[:, 0:1], carry[:, cb:cb + 1])
            g.cp("act", pS[:, 1:n + 1], ps_v)
            g.cp("pool", carry[:, cb:cb + 1], pS[:, n:n + 1])
            g.tt("pool", dtmp[:, 0:n], pS[:, 0:n], pS[:, 1:n + 1], ALU.subtract)
            g.stt(out, dtmp[:, 0:n], mu, pS[:, 1:n + 1], ALU.mult, ALU.add)
        else:
            p3 = pS[:, 0:NSEQ * 9].re("p (j i) -> p j i", i=9)
            d3 = dtmp[:, 0:n].re("p (j i) -> p j i", i=DSEQ)
            o3 = out.re("p (j i) -> p j i", i=DSEQ)
            g.cp("pool", p3[:, :, 0], shS[:, cb, :])
            g.cp("act", p3[:, :, 1:9], ps_v.re("p (j i) -> p j i", i=DSEQ))
            g.cp("pool", shO[:, cb, :], p3[:, :, 8])
            g.tt("pool", d3, p3[:, :, 0:8], p3[:, :, 1:9], ALU.subtract)
            g.stt(o3, d3, mu, p3[:, :, 1:9], ALU.mult, ALU.add)

    def phase_L(l, grp):
        Wv = load_slab(l, [3 * W_A])
        for blk in grp:
            n = blk.n
            pp = nps()
            proj(Wv, 0, blk, pp[:, 0:n])
            sh = Fb[2][:, 0:n]
            shift(pp[:, 0:n], Fb[0], Fb[1], 24, blk, sh)
            g.act(lin[0:64, blk.g0:blk.g0 + n], sh[0:64], AF.Tanh)
            g.cp("pool", lin[64:128, blk.g0:blk.g0 + n], sh[64:128])

    d16 = CB("d16").re("p (a b) -> p a b", a=16)
    kapb = [SB("a_kapb%d" % i, [128, 256], BF16) for i in range(2)]
    rtb = [SB("a_rtb%d" % i, [128, 256], BF16) for i in range(2)]
    gCb = [SB("a_gC%d" % i, [128, 16], F32) for i in range(2)]
    bonb = [SB("a_bon%d" % i, [128, 260], F32) for i in range(2)]
    szb = [SB("a_sz%d" % i, [128, 260], F32) for i in range(2)]
    sbd = [SB("a_sbd%d" % i, [128, 4, 128], F32) for i in range(2)]
    for s_ in sbd:
        g.memset("pool", s_, 0.0)
    Zbd = SB("a_zbd", [128, NSEQ, 128], BF16)
    HS = [slice(0, 64), slice(64, 128)]

    def load_rwkv_state(l, hp):
        for q in range(4):
            S_ = sbd[q % 2]
            for h in range(2):
                g.dma("sp", S_[HS[h], :, HS[h]],
                      st_rwkv[l, 4 * q:4 * q + 4, 2 * hp + h].rearrange("s v k -> v s k"),
                      group=True)
            pT = nps()
            for jj in range(4):
                g.tr(pT[:, jj * 128:(jj + 1) * 128], S_[:, jj], identF)
            p3 = pT.re("p (a b) -> p a b", a=4)
            g.cp(g.ev(), Zbd[:, 4 * q:4 * q + 4], p3)

    def rwkv_stage1(l, hp, blk, par):
        kind = blk.kind
        P = kind == "P"
        nlev = 6 if P else 2
        fl = lambda v: v.re("p a b -> p (a b)")
        hm = CB("hm").bc(2, [128, 2, 128])
        T = []
        for ti in range(blk.ntile):
            c0 = ti * 128
            csl = slice(c0, c0 + 128)
            kt, kh, bt, bh, vb = (Bb[i][:, csl] for i in (2, 3, 4, 5, 8))
            kap, rt = kapb[par][:, csl], rtb[par][:, csl]
            sfx = "_%d_%d" % (ti, par)
            pb = npsb()
            g.tr(pb[:, 0:128], vb, identB)
            yield
            g.tr(pb[:, 128:256], kh, identB)
            yield
            g.tr(pb[:, 256:384], bh, identB)
            yield
            tm = SM("a_tm" + sfx, [128, 3, 128], BF16)
            g.cp("act", tm, pb[:, 0:384].re("p (a b) -> p a b", a=3))
            yield
            bd = {}
            for nm, src in (("kap", kap), ("rt", rt), ("bt", bt)):
                d_ = SM("a_bd_" + nm, [128, 2, 128], BF16)
                g.tt("poolx", d_, src.bc(1, [128, 2, 128]), hm, ALU.mult)
                yield
                bd[nm] = d_.re("p a b -> p (a b)")
            pA, pB, pC = nps(), nps(), nps()
            g.mm(pA[:, 0:256], kt, bd["kap"])
            yield
            g.mm(pA[:, 256:512], kt, bd["rt"])
            yield
            g.mm(pB[:, 0:256], bt, bd["rt"])
            yield
            g.mm(pB[:, 256:512], bt, bd["kap"])
            yield
            g.mm(pC[:, 0:256], kap, bd["bt"])
            yield
            AKB = SM("a_akb" + sfx, [128, 4, 128], BF16)
            BX = SM("a_bx" + sfx, [128, 4, 128], BF16)
            X0 = SM("a_x0_%d" % ti, [128, 2, 128], BF16)
            g.tt("dve", fl(AKB), pA, CB("m4a_" + kind), ALU.mult)
            yield
            g.tt("dve", fl(BX), pB, CB("m4b_" + kind), ALU.mult)
            yield
            g.tt("dve", fl(X0), pC[:, 0:256], CB("m2c_" + kind), ALU.mult)
            yield
            PT = SM("a_pt" + sfx, [128, 2, 128], BF16)
            g.tt("pool", PT, BX[:, 2:4], CB("i2").re("p (a b) -> p a b", a=2), ALU.add)
            yield
            T.append(dict(csl=csl, tm=tm, AKB=AKB, BX=BX, PT=PT, kap=kap, rt=rt,
                          X=[(X0, BX[:, 2:4])], sfx="_%d" % ti))
            yield
        for step in range(1, nlev + 2):
            for t in T:
                Xp, XTp = t["X"][step - 1] if step - 1 < len(t["X"]) else (None, None)
                PT = t["PT"]
                pX = nps()
                need_sq = step <= nlev
                need_T = step < nlev
                if need_sq:
                    for h in range(2):
                        g.mm(pX[:, h * 128:(h + 1) * 128], XTp[:, h], Xp[:, h])
                        yield
                    if need_T:
                        for h in range(2):
                            g.mm(pX[:, (2 + h) * 128:(3 + h) * 128], Xp[:, h], XTp[:, h])
                            yield
                if step >= 2:
                    pP = nps()
                    for h in range(2):
                        g.mm(pP[:, h * 128:(h + 1) * 128], Xp[:, h], PT[:, h])
                        yield
                if need_sq:
                    Xn = SM("a_xn%d%s" % (step % 2, t["sfx"]), [128, 4, 128], BF16)
                    w = 512 if need_T else 256
                    g.cp("act", fl(Xn)[:, 0:w], pX[:, 0:w])
                    yield
                    t["X"].append((Xn[:, 0:2], Xn[:, 2:4]))
                if step >= 2:
                    g.tt("dve", fl(PT), fl(PT), pP[:, 0:256], ALU.add)
            yield
        return T

    def rwkv_tile_stage2(l, hp, blk, ti, t):
        kind = blk.kind
        P = kind == "P"
        csl = t["csl"]
        tm, AKB, BX, PT, kap, rt = t["tm"], t["AKB"], t["BX"], t["PT"], t["kap"], t["rt"]
        Vt, khT, bhT = tm[:, 0], tm[:, 1], tm[:, 2]
        gC = t["gC"]
        pR, pY = nps(), nps()

        def head_groups(pO, opnd, name, tails):
            oms = []
            if not P:
                o4 = opnd.re("p (j i) -> p j i", i=DSEQ).bc(1, [128, 4, NSEQ, DSEQ])
                for q in range(4):
                    om = SM("%s%d" % (name, q), [128, 4, 128], BF16)
                    g.tt("poolx", om.re("p a (j i) -> p a j i", i=DSEQ), o4,
                         d16[:, 4 * q:4 * q + 4, :].bc(3, [128, 4, NSEQ, DSEQ]), ALU.mult)
                    oms.append(om)
            for h in range(2):
                if P:
                    g.mm(pO[:, HS[h]], opnd, Tbd[:, hp][:, HS[h]], start=True, stop=False)
                else:
                    for q in range(4):
                        for jj in range(4):
                            g.mm(pO[:, HS[h]], oms[q][:, jj], Zbd[:, 4 * q + jj][:, HS[h]],
                                 start=(q == 0 and jj == 0), stop=False)
                tl = tails(h)
                for i, (a_, b_) in enumerate(tl):
                    g.mm(pO[:, HS[h]], a_, b_, start=False, stop=(i == len(tl) - 1))

        head_groups(pR, kap, "a_msk", lambda h: [(AKB[:, h], Vt[:, HS[h]])])
        yield
        R1 = SM("a_r1", [128, 128], BF16)
        g.cp("act", R1, pR[:, 0:128])
        yield
        pU = nps()
        for h in range(2):
            g.mm(pU[:, HS[h]], PT[:, h], R1[:, HS[h]])
            yield
        Un = SM("a_un", [128, 128], BF16)
        g.act(Un, pU[:, 0:128], AF.Identity, scale=-1.0)
        yield
        head_groups(pY, rt, "a_msk", lambda h: [(AKB[:, 2 + h], Vt[:, HS[h]]),
                                                (BX[:, h], Un[:, HS[h]])])
        yield
        Ysb = SM("a_ysb", [128, 128], F32)
        g.cp("act", Ysb, pY[:, 0:128])
        yield
        if P:
            pZ = nps()
            g.mm(pZ[:, 0:128], khT, Vt, start=True, stop=False)
            yield
            g.mm(pZ[:, 0:128], bhT, Un, start=False, stop=True)
            yield
            Wz = SM("a_wz", [128, 128], F32)
            g.tt("dve", Wz, pZ[:, 0:128], BO, ALU.mult)
            yield
            g.stt(T32[:, hp], T32[:, hp], gC[:, ti:ti + 1], Wz, ALU.mult, ALU.add)
            yield
            g.cp("act", Tbd[:, hp], T32[:, hp])
            yield
        else:
            rmS = CB("rm_S")
            for q in range(4):
                khTm = SM("a_khTm", [128, 4, 128], BF16)
                bhTm = SM("a_bhTm", [128, 4, 128], BF16)
                rmq = rmS[:, 4 * q:4 * q + 4].bc(2, [128, 4, 128])
                g.tt("poolx", khTm, khT.bc(1, [128, 4, 128]), rmq, ALU.mult)
                g.tt("poolx", bhTm, bhT.bc(1, [128, 4, 128]), rmq, ALU.mult)
                pZ = nps()
                for jj in range(4):
                    g.mm(pZ[:, jj * 128:(jj + 1) * 128], khTm[:, jj], Vt, start=True, stop=False)
                    g.mm(pZ[:, jj * 128:(jj + 1) * 128], bhTm[:, jj], Un, start=False, stop=True)
                Wq = SM("q4f_0", [128, 4, 128], F32)
                g.tt("dve", Wq, pZ.re("p (a b) -> p a b", a=4), BO.bc(1, [128, 4, 128]), ALU.mult)
                S_ = sbd[q % 2]
                for h in range(2):
                    g.dma("sp", S_[HS[h], :, HS[h]],
                          st_rwkv[l, 4 * q:4 * q + 4, 2 * hp + h].rearrange("s v k -> v s k"),
                          group=True)
                pT0 = nps()
                for jj in range(4):
                    g.tr(pT0[:, jj * 128:(jj + 1) * 128], S_[:, jj], identF)
                tq = SM("q4f_1", [128, 4, 128], F32)
                g.tt("dve", tq, pT0.re("p (a b) -> p a b", a=4),
                     gC[:, 4 * q:4 * q + 4].bc(2, [128, 4, 128]), ALU.mult)
                g.tt("pool", tq, tq, Wq, ALU.add)
                pT = nps()
                for jj in range(4):
                    g.tr(pT[:, jj * 128:(jj + 1) * 128], tq[:, jj], identF)
                So = SM("q4f_2", [128, 4, 128], F32)
                g.cp("act", So, pT.re("p (a b) -> p a b", a=4))
                for h in range(2):
                    g.dma("sp", So[HS[h], :, HS[h]],
                          o_s_rwkv[l, 4 * q:4 * q + 4, 2 * hp + h].rearrange("s v k -> v s k"),
                          load=False)
        s1 = SM("a_s1", [128, 2], F32)
        s2 = SM("a_s2", [128, 2], F32)
        mean = SM("a_mean", [128, 2], F32)
        msq = SM("a_msq", [128, 2], F32)
        var = SM("a_var", [128, 2], F32)
        nmr = SM("a_nmr", [128, 2], F32)
        g.rsum(s1, Ysb.re("p (h v) -> p h v", h=2))
        yield
        g.memset("pool", s2, 0.0)
        yield
        for h in range(2):
            g.act(SM("a_jk", [128, 64], BF16), Ysb[:, HS[h]], AF.Square, accum=s2[:, h:h + 1])
            yield
        g.ts("dve", mean, s1, 1.0 / 64, None, ALU.mult)
        yield
        g.tt("dve", msq, mean, mean, ALU.mult)
        yield
        g.stt(var, s2, 1.0 / 64, msq, ALU.mult, ALU.subtract)
        yield
        g.act(var, var, AF.Ln, bias=GN_EPS)
        yield
        g.act(var, var, AF.Exp, scale=-0.5)
        yield
        g.stt(nmr, mean, -1.0, var, ALU.mult, ALU.mult)
        yield
        yn = SM("a_yn", [128, 128], BF16)
        for h in range(2):
            g.act(yn[:, HS[h]], Ysb[:, HS[h]], AF.Identity, bias=nmr[:, h:h + 1], scale=var[:, h:h + 1])
            yield
        pb2 = npsb()
        g.tr(pb2[:, 0:128], yn, identB)
        yield
        g.ts("dve", Fb[23][:, csl], pb2[:, 0:128], pcol("ln_w", hp), pcol("ln_b", hp), ALU.mult, ALU.add)
        yield

    def prep_A(l, hp, Wv, blk, par):
        n = blk.n
        P = blk.kind == "P"
        C = 128 if P else DSEQ
        nch = n // C
        gsl = slice(blk.g0, blk.g0 + n)
        F = lambda i: Fb[i][:, 0:n]
        B = lambda i: Bb[i][:, 0:n]
        wsl = slice(hp * 128, (hp + 1) * 128)
        pp = nps()
        proj(Wv, 0, blk, pp[:, 0:n])
        yield
        shift(pp[:, 0:n], Fb[0], Fb[3], hp, blk, F(4))
        yield
        pp = nps()
        proj(Wv, 1, blk, pp[:, 0:n])
        yield
        shift(pp[:, 0:n], Fb[0], Fb[3], 8 + hp, blk, F(5))
        yield
        pp = nps()
        proj(Wv, 2, blk, pp[:, 0:n])
        yield
        shift(pp[:, 0:n], Fb[0], Fb[3], 16 + hp, blk, F(6))
        yield
        pp = nps()
        proj(Wv, 3, blk, pp[:, 0:n])
        yield
        g.act(szb[par][:, 0:n], pp[:, 0:n], AF.Tanh, scale=0.5)
        yield
        g.stt(szb[par][:, 0:n], szb[par][:, 0:n], 1.0, pp[:, 0:n], ALU.add, ALU.mult)
        yield
        pp = nps()
        g.mm(pp[:, 0:n], w2a2[0:64, wsl], lin[0:64, gsl])
        yield
        g.act(F(8), pp[:, 0:n], AF.Tanh, bias=hwa[:, hp:hp + 1], scale=0.5)
        yield
        g.ts("dve", F(8), F(8), 0.5, 0.5, ALU.mult, ALU.add)
        yield
        scm = (CF("sc128") if P else CF("sc8"))[:, 0:n]
        g.scan(F(9), scm, F(8), 0.0, ALU.mult, ALU.add)
        yield
        g.tt("pool", F(10), F(9), F(8), ALU.subtract)
        yield
        g.act(F(11), F(9), AF.Exp, scale=-CDEC)
        yield
        g.act(F(12), F(9), AF.Exp, scale=CDEC)
        yield
        g.act(F(13), F(10), AF.Exp, scale=-CDEC)
        yield
        cs3 = F(9).re("p (c i) -> p c i", i=C)
        g.act(gCb[par][:, 0:nch], cs3[:, :, C - 1], AF.Exp, scale=-CDEC)
        yield
        g.tt("pool", F(10).re("p (c i) -> p c i", i=C), cs3[:, :, C - 1].bc(2, [128, nch, C]),
             cs3, ALU.subtract)
        yield
        g.act(F(14), F(10), AF.Exp, scale=-CDEC)
        yield
        pp = nps()
        g.mm(pp[:, 0:n], w2a2[64:128, wsl], lin[64:128, gsl])
        yield
        g.act(F(15), pp[:, 0:n], AF.Tanh, bias=hwa[:, 8 + hp:9 + hp], scale=0.5)
        yield
        g.ts("dve", F(15), F(15), 0.5, 0.5, ALU.mult, ALU.add)
        yield
        g.ts("pool", F(16), F(5), pcol("k_k", hp), None, ALU.mult)
        yield
        g.tt("pool", B(0), F(16), F(16), ALU.mult)
        yield
        pp = nps()
        g.mm(pp[:, 0:n], BO, B(0))
        yield
        g.act(F(17), pp[:, 0:n], AF.Ln, bias=1e-24)
        yield
        g.act(F(17), F(17), AF.Exp, scale=-0.5)
        yield
        g.tt("pool", F(18), F(16), F(17), ALU.mult)
        yield
        g.ts("dve", F(19), F(15), 1.0, pcol("k_a", hp), ALU.subtract, ALU.mult)
        yield
        g.stt(F(20), F(19), 1.0, F(5), ALU.add, ALU.mult)
        yield
        g.tt("pool", F(21), F(18), F(15), ALU.mult)
        yield
        g.stt(B(1), F(4), pcol("r_k", hp), F(20), ALU.mult, ALU.mult)
        yield
        pp = nps()
        g.mm(pp[:, 0:n], BO, B(1))
        yield
        g.tt("dve", bonb[par][:, 0:n], pp[:, 0:n], F(6), ALU.mult)
        yield
        g.tt("poolx", B(2), F(20), F(12), ALU.mult)
        yield
        g.tt("poolx", B(3), F(20), F(14), ALU.mult)
        yield
        g.tt("poolx", B(4), F(21), F(12), ALU.mult)
        yield
        g.tt("poolx", B(5), F(21), F(14), ALU.mult)
        yield
        g.tt("pool", kapb[par][:, 0:n], F(18), F(13), ALU.mult)
        yield
        g.tt("pool", rtb[par][:, 0:n], F(4), F(11), ALU.mult)
        yield
        g.cp("pool", B(8), F(6))
        yield
        T = yield from rwkv_stage1(l, hp, blk, par)
        for t in T:
            t["gC"] = gCb[par]
        return dict(T=T, par=par, hp=hp, blk=blk)

    def stage2_A(l, ctx):
        hp, blk, par = ctx["hp"], ctx["blk"], ctx["par"]
        n = blk.n
        gsl = slice(blk.g0, blk.g0 + n)
        if blk.kind == "S":
            load_rwkv_state(l, hp)
            yield
        for ti, t in enumerate(ctx["T"]):
            yield from rwkv_tile_stage2(l, hp, blk, ti, t)
            yield
        ynT = Fb[23][:, 0:n]
        g.tt("pool", ynT, ynT, bonb[par][:, 0:n], ALU.add)
        yield
        g.stt(yaT[:, hp, gsl], ynT, 0.5, szb[par][:, 0:n], ALU.mult, ALU.mult)
        yield

    def out_rwkv_prompt(l, hp):
        pT = nps()
        g.tr(pT[:, 0:128], T32[:, hp], identF)
        So = SM("a_sop", [128, 128], F32)
        g.cp("act", So, pT[:, 0:128])
        for h in range(2):
            g.dma("sp", So[HS[h], HS[h]], o_p_rwkv[l, 2 * hp + h], load=False)

    def tasks_A(l, grp, last_group):
        items = [(hp, bi) for hp in range(8) for bi in range(len(grp))]
        queue, done = [], [0]

        def prep_task():
            Wv = None
            for i, (hp, bi) in enumerate(items):
                while i - done[0] >= 2:
                    yield
                if bi == 0:
                    Wv = load_slab(l, [hp * 128, W_A + hp * 128, 2 * W_A + hp * 128,
                                       OFF_ZA + hp * 128], 0)
                ctx = yield from prep_A(l, hp, Wv, grp[bi], i % 2)
                queue.append(ctx)
                yield

        def stage2_task():
            for i, (hp, bi) in enumerate(items):
                while not queue:
                    yield
                ctx = queue.pop(0)
                yield from stage2_A(l, ctx)
                done[0] += 1
                if bi == len(grp) - 1 and last_group:
                    out_rwkv_prompt(l, hp)
                yield

        return [(prep_task(), POOL_PREP), (stage2_task(), POOL_ST2)]

    def tasks_B(l, grp, last_group):
        def b_task():
            for hbi in range(8):
                yield from task_B(l, grp, hbi, last_group)
        return [(b_task(), POOL_B)]

    def run_tasks(tasks):
        tasks = list(tasks)
        while tasks:
            for t in list(tasks):
                cur_pool[0] = t[1]
                try:
                    next(t[0])
                except StopIteration:
                    tasks.remove(t)
        cur_pool[0] = POOL_ALL

    def hgrn_tile(l, hbi, blk, ti, vtok):
        kind = blk.kind
        P = kind == "P"
        C = 32 if P else DSEQ
        nseg = 128 // C
        c0 = ti * 128
        csl = slice(c0, c0 + 128)
        qt, kt, kh = (BbB[i][:, csl] for i in (0, 1, 2))
        vt = vtok[:, ti]
        gl = FbB[11]
        pA = nps()
        g.mm(pA[:, 0:128], kt, qt)
        yield
        attm = SM("b_attm", [128, 128], BF16)
        g.tt("dve", attm, pA[:, 0:128], CB("mih_" + kind), ALU.mult)
        yield
        pb = npsb()
        g.tr(pb[:, 0:128], kh, identB)
        yield
        khs = SM("b_khs", [128, 128], BF16)
        g.cp("act", khs, pb[:, 0:128])
        yield
        khb = khs.bc(1, [128, 4, 128])
        pO = cur_pool[0]["fixed"]
        if P:
            khm = SM("b_khm", [128, 4, 128], BF16)
            g.tt("poolx", khm, khb, CB("rm_P").bc(2, [128, 4, 128]), ALU.mult)
            yield
            for i in range(nseg):
                ss = slice(C * i, C * i + C)
                g.mm(pO[:, ss], vt, attm[:, ss], start=True, stop=False)
                yield
                g.mm(pO[:, ss], Sbf[:, hbi], qt[:, ss], start=False, stop=True)
                yield
                pS_ = nps()
                g.mm(pS_[:, 0:128], khm[:, i], vt)
                yield
                g.stt(S32[:, hbi], S32[:, hbi], gl[:, ti * nseg + i:ti * nseg + i + 1],
                      pS_[:, 0:128], ALU.mult, ALU.add)
                yield
                g.cp("act", Sbf[:, hbi], S32[:, hbi])
                yield
        else:
            rmS = CB("rm_S")
            for q in range(4):
                Sin = SM("q4f_%d" % (q % 2), [128, 4, 128], F32)
                g.dma("sp", Sin, st_hgrn[l, 4 * q:4 * q + 4, hbi].rearrange("s d e -> d s e"))
                Sib = SM("b_sib", [128, 4, 128], BF16)
                g.cp("act", Sib, Sin)
                khm = SM("b_khm", [128, 4, 128], BF16)
                g.tt("poolx", khm, khb, rmS[:, 4 * q:4 * q + 4].bc(2, [128, 4, 128]), ALU.mult)
                pS_ = nps()
                for jj in range(4):
                    j = 4 * q + jj
                    ss = slice(C * j, C * j + C)
                    g.mm(pO[:, ss], vt, attm[:, ss], start=True, stop=False)
                    g.mm(pO[:, ss], Sib[:, jj], qt[:, ss], start=False, stop=True)
                    g.mm(pS_[:, jj * 128:(jj + 1) * 128], khm[:, jj], vt)
                So = SM("q4f_%d" % (2 + q % 2), [128, 4, 128], F32)
                g.tt("pool", So, Sin, gl[:, 4 * q:4 * q + 4].bc(2, [128, 4, 128]), ALU.mult)
                g.tt("dve", So, So, pS_.re("p (a b) -> p a b", a=4), ALU.add)
                g.dma("sp", So, o_s_hgrn[l, 4 * q:4 * q + 4, hbi].rearrange("s d e -> d s e"),
                      load=False)
                yield
        g.cp("act", FbB[9][:, csl], pO[:, 0:128])
        yield

    def phase_B_block(l, hbi, Wv, blk):
        n = blk.n
        P = blk.kind == "P"
        C = 32 if P else DSEQ
        nch = n // C
        gsl = slice(blk.g0, blk.g0 + n)
        F = lambda i: FbB[i][:, 0:n]
        B = lambda i: BbB[i][:, 0:n]
        pp = nps()
        proj(Wv, 0, blk, pp[:, 0:n])
        yield
        g.act(F(0), pp[:, 0:n], AF.Tanh, scale=0.5)
        yield
        g.stt(F(0), F(0), 1.0, pp[:, 0:n], ALU.add, ALU.mult)
        yield
        pp = nps()
        proj(Wv, 1, blk, pp[:, 0:n])
        yield
        g.act(F(1), pp[:, 0:n], AF.Tanh, scale=0.5)
        yield
        g.ts("dve", F(1), F(1), -0.5, 0.5, ALU.mult, ALU.add)
        yield
        g.ts("dve", F(1), F(1), oml[:, hbi, l:l + 1], MAXK, ALU.mult, ALU.min)
        yield
        g.act(F(2), F(1), AF.Ln, bias=1.0, scale=-1.0)
        yield
        scm = (CF("sc32") if P else CF("sc8"))[:, 0:n]
        g.scan(F(3), scm, F(2), 0.0, ALU.mult, ALU.add)
        yield
        cs3 = F(3).re("p (c i) -> p c i", i=C)
        g.tt("pool", F(4).re("p (c i) -> p c i", i=C), cs3[:, :, C - 1].bc(2, [128, nch, C]),
             cs3, ALU.subtract)
        yield
        g.act(F(5), F(3), AF.Exp)
        yield
        g.act(F(6), F(3), AF.Exp, scale=-1.0)
        yield
        g.act(F(7), F(4), AF.Exp)
        yield
        g.act(FbB[11][:, 0:nch], cs3[:, :, C - 1], AF.Exp)
        yield
        g.stt(B(0), F(0), 0.5, F(5), ALU.mult, ALU.mult)
        yield
        g.tt("pool", B(1), F(1), F(6), ALU.mult)
        yield
        g.tt("pool", B(2), F(1), F(7), ALU.mult)
        yield
        pp = nps()
        proj(Wv, 3, blk, pp[:, 0:n])
        yield
        g.act(F(8), pp[:, 0:n], AF.Tanh, scale=0.5)
        yield
        g.stt(F(8), F(8), 1.0, pp[:, 0:n], ALU.add, ALU.mult)
        yield
        vtok = SM("b_vtok", [128, 2, 128], BF16)
        for ti in range(blk.ntile):
            pv_ = nps()
            col = blk.g0 + ti * 128
            for c in range(KC):
                g.mm(pv_[:, 0:128], hT[:, c, col:col + 128], Wv[:, 2, c, :],
                     start=(c == 0), stop=(c == KC - 1))
                yield
            g.cp(g.ev(), vtok[:, ti], pv_[:, 0:128])
        yield
        for ti in range(blk.ntile):
            yield from hgrn_tile(l, hbi, blk, ti, vtok)
            yield
        g.tt("pool", B(3), F(9), F(9), ALU.mult)
        yield
        pp = nps()
        g.mm(pp[:, 0:n], ONES, B(3))
        yield
        g.act(F(10), pp[:, 0:n], AF.Ln, bias=NORM_EPS, scale=1.0 / 128)
        yield
        g.act(F(10), F(10), AF.Exp, scale=-0.5)
        yield
        g.stt(F(9), F(9), pcol("hg_g", 0), F(10), ALU.mult, ALU.mult)
        yield
        g.stt(ybT[:, hbi, gsl], F(9), 0.5, F(8), ALU.mult, ALU.mult)
        yield

    def task_B(l, grp, hbi, last_group):
        cols = [OFF_QB + hbi * 128, OFF_FB + hbi * 128, OFF_IB + hbi * 128, OFF_ZB + hbi * 128]
        Wv = load_slab(l, cols, 1)
        for blk in grp:
            yield from phase_B_block(l, hbi, Wv, blk)
            yield
        if last_group:
            g.dma("sp", S32[:, hbi], o_p_hgrn[l, hbi], load=False)

    def phase_M(l, grp):
        def load_m(dc):
            W = nW()
            Wga = W[:, 0:2048].re("p (c n) -> p c n", c=KC)
            Wgb = W[:, 2048:4096].re("p (c n) -> p c n", c=KC)
            Wpa = W[:, 4096:5120].re("p (c n) -> p c n", c=8)
            Wpb = W[:, 5120:6144].re("p (c n) -> p c n", c=8)
            dsl = slice(dc * 128, dc * 128 + 128)
            g.dma("pool", Wga, w_in[l, :, OFF_GA + dc * 128:OFF_GA + dc * 128 + 128]
                  .rearrange("(c p) n -> p c n", p=128), group=True)
            g.dma("pool", Wgb, w_in[l, :, OFF_GB + dc * 128:OFF_GB + dc * 128 + 128]
                  .rearrange("(c p) n -> p c n", p=128), group=True)
            g.dma("pool", Wpa, proj_a[l, :, dsl].rearrange("(c p) n -> p c n", p=128), group=True)
            g.dma("pool", Wpb, proj_b[l, :, dsl].rearrange("(c p) n -> p c n", p=128), group=True)
            return Wga, Wgb, Wpa, Wpb

        tot_ = sum(b.n for b in grp)
        spans = [(o_, min(512, tot_ - o_)) for o_ in range(0, tot_, 512)]
        Wn = load_m(0)
        for dc in range(KC):
            Wga, Wgb, Wpa, Wpb = Wn
            if dc < KC - 1:
                Wn = load_m(dc + 1)
            for (g0_, n) in spans:
                gsl = slice(g0_, g0_ + n)
                F = lambda i: SM("q4f_%d" % i, [128, 4, 128], F32).re("p a b -> p (a b)")[:, 0:n]
                p1 = nps()
                for c in range(KC):
                    g.mm(p1[:, 0:n], Wga[:, c], hT[:, c, gsl], start=(c == 0), stop=(c == KC - 1))
                g.act(F(0), p1[:, 0:n], AF.Sigmoid)
                p2 = nps()
                for c in range(8):
                    g.mm(p2[:, 0:n], Wpa[:, c], yaT[:, c, gsl], start=(c == 0), stop=(c == 7))
                g.tt("dve", F(1), p2[:, 0:n], F(0), ALU.mult)
                p3 = nps()
                for c in range(KC):
                    g.mm(p3[:, 0:n], Wgb[:, c], hT[:, c, gsl], start=(c == 0), stop=(c == KC - 1))
                g.act(F(2), p3[:, 0:n], AF.Sigmoid)
                p4 = nps()
                for c in range(8):
                    g.mm(p4[:, 0:n], Wpb[:, c], ybT[:, c, gsl], start=(c == 0), stop=(c == 7))
                g.tt("dve", F(3), p4[:, 0:n], F(2), ALU.mult)
                g.tt("pool", mT[:, dc, gsl], F(1), F(3), ALU.add)

        def load_o(eb):
            W = nW()
            Wo = W.re("p (c n) -> p c n", c=KC)
            g.dma("pool", Wo, w_out[l, :, eb * 512:(eb + 1) * 512].rearrange("(c p) n -> p c n", p=128))
            return Wo

        Wn = load_o(0)
        k = 0
        for eb in range(4):
            Wo = Wn
            if eb < 3:
                Wn = load_o(eb + 1)
            esl = slice(eb * 512, (eb + 1) * 512)
            for blk in grp:
                for ti in range(blk.ntile):
                    row = blk.row0 + ti * 128
                    col = blk.g0 + ti * 128
                    po = nps()
                    for dc in range(KC):
                        g.mm(po, mT[:, dc, col:col + 128], Wo[:, dc], start=(dc == 0), stop=(dc == KC - 1))
                    k += 1
                    xs_ = SM("q4f_%d" % (k % 2), [128, 4, 128], F32).re("p a b -> p (a b)")
                    yt = ytile[row // 128]
                    if l == 0:
                        g.dma("sp", xs_, xin[row:row + 128, esl])
                    else:
                        g.dma("sp", xs_, y[row:row + 128, esl], other=yt)
                    g.tt("dve", xs_, xs_, po, ALU.add)
                    g.dma("sp", xs_, y[row:row + 128, esl], load=False, other=yt)

    def load_shift_state(l):
        for r in range(7):
            nb = min(4, 25 - 4 * r)
            st = SM("q4f_3", [128, 4, 128], F32).re("p a b -> p (a b)")[0:NSEQ, 0:nb * 128]
            g.dma("sp", st, st_shift[l, :, r * 512:r * 512 + nb * 128])
            pT = nps()
            for k_ in range(nb):
                g.tr(pT[:, k_ * NSEQ:(k_ + 1) * NSEQ], st[:, k_ * 128:(k_ + 1) * 128],
                     identF[0:NSEQ, 0:NSEQ])
            g.cp("act", shS[:, 4 * r:4 * r + nb, :].re("p b s -> p (b s)"), pT[:, 0:nb * NSEQ])

    def store_shift_state(l):
        for r in range(7):
            nb = min(4, 25 - 4 * r)
            pT = nps()
            for k_ in range(nb):
                b = 4 * r + k_
                g.tr(pT[0:NSEQ, k_ * 128:(k_ + 1) * 128], shO[:, b, :], identF)
            so = SM("q4f_%d" % (2 + r % 2), [128, 4, 128], F32).re("p a b -> p (a b)")[0:NSEQ, 0:nb * 128]
            g.cp("act", so, pT[0:NSEQ, 0:nb * 128])
            g.dma("sp", so, o_s_shift[l, :, r * 512:r * 512 + nb * 128], load=False)
        pT = nps()
        g.tr(pT[0:25, 0:128], carry, identF)
        sp_ = SM("sh_outp", [25, 128], F32)
        g.cp("act", sp_, pT[0:25, 0:128])
        g.dma("sp", sp_, o_p_shift[l].rearrange("(b p) -> b p", p=128), load=False)

    for l in range(depth):
        if l > 0:
            g.dma("sp", pvt, pv_d[l])
        g.dma("pool", w2a2, w2a2_d[l])
        g.ts("dve", hwa, pvt[:, PV["w0"]:PV["w0"] + 16], 0.5, None, ALU.mult)
        load_shift_state(l)
        for t_ in (T32, S32, carry):
            g.memset("pool", t_, 0.0)
        for t_ in (Tbd, Sbf):
            g.memset("pool", t_, 0.0)
        for gi, grp in enumerate(groups):
            last = gi == len(groups) - 1
            last_p = gi == len(groups) - 2
            import os
            stop = int(os.environ.get("KSTOP", "99"))
            kb.label = "N"
            if stop >= 1:
                phase_N(l, grp)
            kb.label = "L"
            if stop >= 2:
                phase_L(l, grp)
            kb.label = "AB"
            KB.hand_over(_alias_src, _alias_dst)
            run_tasks(tasks_A(l, grp, last_p) + tasks_B(l, grp, last_p))
            KB.hand_over(_alias_dst, _alias_src)
            kb.label = "M"
            if stop >= 5:
                phase_M(l, grp)
        if stop >= 6:
            store_shift_state(l)

    gbc = Wslab[0][:, 0:2 * D].bitcast(F32)
    g.dma("sp", gbc, gbc_d)
    for ti in range(NTOK // 128 if stop >= 7 else 0):
        x = nxt()
        yt = ytile[ti]
        g.dma("sp", x, y[ti * 128:(ti + 1) * 128, :], other=yt)
        rs = rms_rows(x, depth)
        g.stt(x, x, rs, gbc, ALU.mult, ALU.mult)
        g.dma("sp", x, y[ti * 128:(ti + 1) * 128, :], load=False, other=yt)

    kb.finish()
    nc._kb_labels = kb.labels
    return nc, (cF_np, cB_np)


_CACHE = {}


def _f32(a):
    return np.ascontiguousarray(np.asarray(a, dtype=np.float32))


def kernel(x_prompt, x_sample, state_rwkv, state_hgrn, state_shift, norm_g, w_in, shift_mu,
           rwkv_w0, rwkv_w2, rwkv_a0, rwkv_a2, rwkv_k_k, rwkv_k_a, rwkv_r_k, rwkv_ln_w,
           rwkv_ln_b, hgrn_lb_logits, hgrn_norm_g, proj_a, proj_b, w_out, final_norm_g,
           _tiles_per_group=4):
    x_prompt, x_sample = _f32(x_prompt), _f32(x_sample)
    state_rwkv, state_hgrn, state_shift = _f32(state_rwkv), _f32(state_hgrn), _f32(state_shift)
    depth = int(np.asarray(norm_g).shape[0])
    nb, seq = x_prompt.shape[0], x_prompt.shape[1]
    npt = seq // 128
    ncores = x_sample.shape[0] // NSEQ
    key = (depth, npt, _tiles_per_group)
    if key not in _CACHE:
        _CACHE[key] = build(depth, npt, _tiles_per_group)
    nc, (cF, cB) = _CACHE[key]

    w_in, proj_a, proj_b, w_out = _f32(w_in), _f32(proj_a), _f32(proj_b), _f32(w_out)
    w2a2 = np.ascontiguousarray(np.concatenate([_f32(rwkv_w2), _f32(rwkv_a2)], axis=1))
    gbc = np.ascontiguousarray(np.broadcast_to(_f32(final_norm_g)[None, :], (128, D)))
    pv = np.zeros((depth, 128, NPV), np.float32)
    lb = _f32(hgrn_lb_logits).reshape(depth, 8, 128).transpose(2, 1, 0)
    lbp = np.zeros((128, 8, 4), np.float32)
    lbp[:, :, :depth] = lb
    for l in range(depth):
        def put(name, vec, ncol):
            pv[l, :, PV[name]:PV[name] + ncol] = _f32(vec).reshape(ncol, 128).T
        put("mu", shift_mu[l], 25)
        put("w0", rwkv_w0[l], 8)
        put("a0", rwkv_a0[l], 8)
        put("k_k", rwkv_k_k[l], 8)
        put("k_a", rwkv_k_a[l], 8)
        put("r_k", np.asarray(rwkv_r_k[l]).reshape(-1), 8)
        put("ln_w", rwkv_ln_w[l], 8)
        put("ln_b", rwkv_ln_b[l], 8)
        put("hg_g", hgrn_norm_g[l], 1)
        put("ng", norm_g[l], 16)
        pv[l, :, PV["lbl"]:PV["lbl"] + 32] = lbp.reshape(128, 32)

    in_maps = []
    for c in range(ncores):
        sl = slice(NSEQ * c, NSEQ * (c + 1))
        xin = np.concatenate([x_prompt[c % nb], x_sample[sl].reshape(NSEQ * DSEQ, D)], axis=0)
        in_maps.append({
            "xin": np.ascontiguousarray(xin),
            "st_rwkv": np.ascontiguousarray(state_rwkv[:, sl]),
            "st_hgrn": np.ascontiguousarray(state_hgrn[:, sl]),
            "st_shift": np.ascontiguousarray(state_shift[:, sl]),
            "w_in": w_in, "proj_a": proj_a, "proj_b": proj_b, "w_out": w_out,
            "w2a2": w2a2, "gbc": gbc, "pv": pv, "cstF": cF, "cstB": cB,
        })
    res = run_bass_kernel_spmd(nc, in_maps, core_ids=list(range(ncores))).results
    NP = npt * 128
    y_prompt = np.stack([res[b]["y"][:NP] for b in range(nb)], axis=0)
    y_sample = np.concatenate([res[c]["y"][NP:].reshape(NSEQ, DSEQ, D) for c in range(ncores)], axis=0)
    p_ra = np.stack([res[b]["o_p_rwkv"] for b in range(nb)], axis=1)
    p_hb = np.stack([res[b]["o_p_hgrn"] for b in range(nb)], axis=1)
    p_sh = np.stack([res[b]["o_p_shift"] for b in range(nb)], axis=1)
    s_ra = np.concatenate([res[c]["o_s_rwkv"] for c in range(ncores)], axis=1)
    s_hb = np.concatenate([res[c]["o_s_hgrn"] for c in range(ncores)], axis=1)
    s_sh = np.concatenate([res[c]["o_s_shift"] for c in range(ncores)], axis=1)
    return tuple(np.ascontiguousarray(a, dtype=np.float32)
                 for a in (y_prompt, y_sample, p_ra, p_hb, p_sh, s_ra, s_hb, s_sh))
```
